# Optimizing a Trainium2 kernel written in Bass

```python
import math
import jax, jax.numpy as jnp
from jax import lax
import numpy as np

D_MODEL = 1024
BATCH = 4
SEQ = 4096
DEPTH = 2

HEAD_DIM = 64
EPS = 1e-6
DA_HEADS = 4
DA_QK = HEAD_DIM
DA_V = 2 * HEAD_DIM
DA_WIDTH = DA_HEADS * DA_V
Q_BLOCK = 128
ROPE_THETA = 500000.0
ROT_DIM = HEAD_DIM // 4
SC_WIDTH = 256
SC_GROUPS = 4
CONV_W = 3
GLA_HEADS = 4
GLA_DK = 32
GLA_DV = 64
GLA_KW = GLA_HEADS * GLA_DK
GLA_VW = GLA_HEADS * GLA_DV
GLA_GATE_RANK = 16
GLA_GATE_TEMP = 16.0
GLA_CHUNK = 64
MIX_WIDTH = DA_WIDTH + SC_WIDTH + GLA_VW
IN_SIZES = (DA_HEADS * 2 * DA_QK, DA_HEADS * 2 * DA_QK, DA_WIDTH,
            SC_WIDTH, SC_WIDTH, SC_WIDTH,
            GLA_KW, GLA_KW, GLA_VW, GLA_VW, GLA_GATE_RANK)
IN_WIDTH = 3088
PEER_HEADS = 8
PEER_NKEYS = 128
PEER_EXPERTS = PEER_NKEYS * PEER_NKEYS
PEER_HALF = 128
PEER_QDIM = 2 * PEER_HALF
PEER_TOPK = 16
PEER_TOKEN_BLOCK = 128

kernel_name = "hymba_style_diffattn_shortconv_gla_peer"


def rmsnorm(x, g):
    xf = x.astype(jnp.float32)
    y = xf * lax.rsqrt(jnp.mean(xf * xf, axis=-1, keepdims=True) + EPS)
    return (y * g.astype(jnp.float32)).astype(x.dtype)


def rope_tables(positions):
    inv = ROPE_THETA ** (-jnp.arange(0, ROT_DIM, 2, dtype=jnp.float32) / ROT_DIM)
    ang = positions.astype(jnp.float32)[..., None] * inv
    return jnp.cos(ang), jnp.sin(ang)


def apply_partial_rope(t, cos, sin):
    rot, rest = t[..., :ROT_DIM], t[..., ROT_DIM:]
    r1, r2 = rot[..., :ROT_DIM // 2], rot[..., ROT_DIM // 2:]
    c = cos[:, :, None, None, :].astype(t.dtype)
    s = sin[:, :, None, None, :].astype(t.dtype)
    return jnp.concatenate([r1 * c - r2 * s, r1 * s + r2 * c, rest], axis=-1)


def diff_attention(q, k, v, lam, out_scale, gain):
    B, S = q.shape[0], q.shape[1]
    nblk = S // Q_BLOCK
    qb = q.reshape(B, nblk, Q_BLOCK, DA_HEADS, 2, DA_QK).transpose(1, 0, 2, 3, 4, 5)
    kpos = jnp.arange(S)
    scale = DA_QK ** -0.5

    def block(args):
        qi, bi = args
        s = jnp.einsum('bqhmd,bkhmd->bhmqk', qi, k).astype(jnp.float32) * scale
        qpos = bi * Q_BLOCK + jnp.arange(Q_BLOCK)
        mask = kpos[None, :] <= qpos[:, None]
        p = jax.nn.softmax(jnp.where(mask, s, -jnp.inf), axis=-1)
        a = p[:, :, 0] - lam * p[:, :, 1]
        return jnp.einsum('bhqk,bkhe->bqhe', a.astype(v.dtype), v)

    o = lax.map(block, (qb, jnp.arange(nblk)))
    o = o.transpose(1, 0, 2, 3, 4).reshape(B, S, DA_HEADS, DA_V)
    o = rmsnorm(o, gain) * out_scale
    return o.reshape(B, S, DA_WIDTH)


def short_conv(b_gate, c_gate, h_in, w):
    z = c_gate * h_in
    S = z.shape[1]
    zp = jnp.pad(z, ((0, 0), (CONV_W - 1, 0), (0, 0)))
    y = w[0] * zp[:, 0:S]
    for j in range(1, CONV_W):
        y = y + w[j] * zp[:, j:j + S]
    return b_gate * y


def gla(q, k, v, log_a):
    B, S = q.shape[0], q.shape[1]
    nc = S // GLA_CHUNK

    def chunks(t):
        return t.astype(jnp.float32).reshape(B, nc, GLA_CHUNK, GLA_HEADS, t.shape[-1]).transpose(1, 0, 3, 2, 4)

    qc, kc, vc, ac = chunks(q * (GLA_DK ** -0.5)), chunks(k), chunks(v), chunks(log_a)
    causal = jnp.tril(jnp.ones((GLA_CHUNK, GLA_CHUNK), dtype=bool))

    def step(state, inp):
        qi, ki, vi, ai = inp
        b = lax.cumsum(ai, axis=2)
        diff = b[:, :, :, None, :] - b[:, :, None, :, :]
        decay = jnp.exp(jnp.where(causal[:, :, None], diff, -jnp.inf))
        attn = jnp.sum(qi[:, :, :, None, :] * ki[:, :, None, :, :] * decay, axis=-1)
        o = jnp.einsum('bhij,bhje->bhie', attn, vi) + jnp.einsum('bhid,bhde->bhie', qi * jnp.exp(b), state)
        b_last = b[:, :, -1:, :]
        state = jnp.exp(b_last[:, :, 0, :, None]) * state + jnp.einsum(
            'bhjd,bhje->bhde', ki * jnp.exp(b_last - b), vi)
        return state, o

    state0 = jnp.zeros((B, GLA_HEADS, GLA_DK, GLA_DV), jnp.float32)
    _, o = lax.scan(step, state0, (qc, kc, vc, ac))
    o = o.transpose(1, 0, 3, 2, 4).reshape(B, S, GLA_HEADS, GLA_DV)
    return o.astype(v.dtype)


def peer(xn, w_q, keys1, keys2, u, v):
    B, S, D = xn.shape
    T = B * S
    xt = xn.reshape(T, D)
    q = (xt @ w_q).reshape(T, PEER_HEADS, 2, PEER_HALF)
    s1 = jnp.einsum('thd,nd->thn', q[:, :, 0], keys1).astype(jnp.float32)
    s2 = jnp.einsum('thd,nd->thn', q[:, :, 1], keys2).astype(jnp.float32)
    v1, i1 = lax.top_k(s1, PEER_TOPK)
    v2, i2 = lax.top_k(s2, PEER_TOPK)
    cand = (v1[..., :, None] + v2[..., None, :]).reshape(T, PEER_HEADS, PEER_TOPK * PEER_TOPK)
    sc, ci = lax.top_k(cand, PEER_TOPK)
    e1 = jnp.take_along_axis(i1, ci // PEER_TOPK, axis=-1)
    e2 = jnp.take_along_axis(i2, ci % PEER_TOPK, axis=-1)
    hk = PEER_HEADS * PEER_TOPK
    experts = (e1 * PEER_NKEYS + e2).reshape(T, hk)
    gates = jax.nn.softmax(sc, axis=-1).reshape(T, hk).astype(xn.dtype)
    nb = T // PEER_TOKEN_BLOCK

    def block(args):
        xb, eb, gb = args
        ub = jnp.take(u, eb, axis=0)
        hb = jax.nn.gelu(jnp.einsum('tkd,td->tk', ub, xb), approximate=False)
        vb = jnp.take(v, eb, axis=0)
        return jnp.einsum('tk,tkd->td', gb * hb, vb)

    out = lax.map(block, (xt.reshape(nb, PEER_TOKEN_BLOCK, D),
                          experts.reshape(nb, PEER_TOKEN_BLOCK, hk),
                          gates.reshape(nb, PEER_TOKEN_BLOCK, hk)))
    return out.reshape(B, S, D)


def setup_inputs(seed: int = 0) -> dict:
    key = jax.random.key(seed)
    ks = jax.random.split(key, 24)
    f32 = jnp.float32
    L, D = DEPTH, D_MODEL
    nrm = lambda k, shape, s: jax.random.normal(k, shape, f32) * s
    return {
        "x": jax.random.normal(ks[0], (BATCH, SEQ, D), f32),
        "positions": jnp.broadcast_to(jnp.arange(SEQ, dtype=jnp.int32), (BATCH, SEQ)),
        "norm_mix": 1.0 + nrm(ks[1], (L, D), 0.02),
        "w_in": nrm(ks[2], (L, D, IN_WIDTH), D ** -0.5),
        "lam_q1": nrm(ks[3], (L, DA_QK), 0.1),
        "lam_k1": nrm(ks[4], (L, DA_QK), 0.1),
        "lam_q2": nrm(ks[5], (L, DA_QK), 0.1),
        "lam_k2": nrm(ks[6], (L, DA_QK), 0.1),
        "diff_norm": 1.0 + nrm(ks[7], (L, DA_V), 0.02),
        "conv_w": nrm(ks[8], (L, CONV_W, SC_WIDTH), CONV_W ** -0.5),
        "gla_w_gate2": nrm(ks[9], (L, GLA_GATE_RANK, GLA_KW), GLA_GATE_RANK ** -0.5),
        "gla_b_gate": nrm(ks[10], (L, GLA_KW), 0.1),
        "gla_norm": 1.0 + nrm(ks[11], (L, GLA_DV), 0.02),
        "w_out": nrm(ks[12], (L, MIX_WIDTH, D), MIX_WIDTH ** -0.5),
        "norm_ffn": 1.0 + nrm(ks[13], (L, D), 0.02),
        "peer_w_q": nrm(ks[14], (L, D, PEER_HEADS * PEER_QDIM), D ** -0.5),
        "peer_keys1": nrm(ks[15], (L, PEER_NKEYS, PEER_HALF), PEER_HALF ** -0.5),
        "peer_keys2": nrm(ks[16], (L, PEER_NKEYS, PEER_HALF), PEER_HALF ** -0.5),
        "peer_u": nrm(ks[17], (L, PEER_EXPERTS, D), D ** -0.5),
        "peer_v": nrm(ks[18], (L, PEER_EXPERTS, D), (PEER_HEADS * PEER_TOPK) ** -0.5),
        "norm_final": 1.0 + nrm(ks[19], (D,), 0.02),
    }


def reference(x, positions, norm_mix, w_in, lam_q1, lam_k1, lam_q2, lam_k2, diff_norm,
              conv_w, gla_w_gate2, gla_b_gate, gla_norm, w_out, norm_ffn, peer_w_q,
              peer_keys1, peer_keys2, peer_u, peer_v, norm_final):
    B, S, _ = x.shape
    cos, sin = rope_tables(positions)
    split_points = [int(p) for p in np.cumsum(IN_SIZES)[:-1]]
    for i in range(DEPTH):
        h = rmsnorm(x, norm_mix[i])
        proj = h @ w_in[i]
        (da_q, da_k, da_v, sc_b, sc_c, sc_h,
         g_q, g_k, g_v, g_r, g_lr) = jnp.split(proj, split_points, axis=-1)

        lam_init = 0.8 - 0.6 * math.exp(-0.3 * i)
        lam = (jnp.exp(jnp.sum(lam_q1[i].astype(jnp.float32) * lam_k1[i].astype(jnp.float32)))
               - jnp.exp(jnp.sum(lam_q2[i].astype(jnp.float32) * lam_k2[i].astype(jnp.float32)))
               + lam_init)
        q = apply_partial_rope(da_q.reshape(B, S, DA_HEADS, 2, DA_QK), cos, sin)
        k = apply_partial_rope(da_k.reshape(B, S, DA_HEADS, 2, DA_QK), cos, sin)
        o_da = diff_attention(q, k, da_v.reshape(B, S, DA_HEADS, DA_V), lam,
                              1.0 - lam_init, diff_norm[i])

        o_sc = short_conv(sc_b, sc_c, sc_h, conv_w[i])

        log_a = jax.nn.log_sigmoid((g_lr @ gla_w_gate2[i] + gla_b_gate[i]).astype(jnp.float32)) / GLA_GATE_TEMP
        o_g = gla(g_q.reshape(B, S, GLA_HEADS, GLA_DK), g_k.reshape(B, S, GLA_HEADS, GLA_DK),
                  g_v.reshape(B, S, GLA_HEADS, GLA_DV), log_a.reshape(B, S, GLA_HEADS, GLA_DK))
        o_g = rmsnorm(o_g, gla_norm[i]) * jax.nn.silu(g_r.reshape(B, S, GLA_HEADS, GLA_DV))
        o_g = o_g.reshape(B, S, GLA_VW)

        mixed = jnp.concatenate([o_da, o_sc, o_g], axis=-1)
        x = x + mixed @ w_out[i]

        x = x + peer(rmsnorm(x, norm_ffn[i]), peer_w_q[i], peer_keys1[i], peer_keys2[i],
                     peer_u[i], peer_v[i])
    return rmsnorm(x, norm_final)
```

```python
import numpy as np
import concourse.bass as bass
import concourse.mybir as mybir

F32 = mybir.dt.float32
BF16 = mybir.dt.bfloat16
I32 = mybir.dt.int32
U32 = mybir.dt.uint32
AF = mybir.ActivationFunctionType
ALU = mybir.AluOpType
AX = mybir.AxisListType

EPOCH = 8000
WRITE_KEYS = ("out", "accum_out", "out_max", "out_indices", "ap")


class Buf:
    def __init__(self, S, tensor, name):
        self.S = S
        self.tensor = tensor
        self.name = name
        self.w = {}
        self.r = []

    def __getitem__(self, key):
        return View(self, self.tensor[key])

    def v(self, ap):
        return View(self, ap)


class View:
    def __init__(self, buf, ap):
        self.buf = buf
        self.ap = ap

    def __getitem__(self, key):
        return View(self.buf, self.ap[key])

    def rearrange(self, s, **kw):
        return View(self.buf, self.ap.rearrange(s, **kw))

    def bcast(self, shape):
        return View(self.buf, self.ap.to_broadcast(shape))

    def bitcast(self, dt):
        return View(self.buf, self.ap.bitcast(dt))


class Counter:
    def __init__(self, S, name, unit):
        self.S = S
        self.name = name
        self.unit = unit
        self.n = 0
        self.sems = []
        self.per_epoch = EPOCH // unit

    def sem_for(self, n):
        ep = (n - 1) // self.per_epoch
        while len(self.sems) <= ep:
            self.sems.append(self.S.nc.alloc_semaphore(name=f"{self.name}_e{len(self.sems)}"))
        return self.sems[ep], ((n - 1) % self.per_epoch + 1) * self.unit


ENGS = ("pe", "act", "dve", "pool", "sp")


class Sched:
    def __init__(self, nc, n_dma_slots=8):
        self.nc = nc
        self.cnt = {e: Counter(self, e, 1) for e in ENGS}
        self.prog = {e: [] for e in ENGS}
        self.known = {e: {} for e in ENGS}
        self.dma_slots = {}
        self.dma_rr = {}
        for q in ("sp", "pool", "act"):
            self.dma_slots[q] = [Counter(self, f"dma_{q}{i}", 16) for i in range(n_dma_slots)]
            self.dma_rr[q] = 0
        self.counters = {c.name: c for c in self.cnt.values()}
        for q in self.dma_slots:
            for c in self.dma_slots[q]:
                self.counters[c.name] = c
        self.stack = []

    def _need(self, eng, dep):
        if dep is None:
            return
        cname, n = dep
        if n <= 0:
            return
        if cname == "pe" and eng == "pe":
            return
        k = self.known[eng]
        if k.get(cname, 0) >= n:
            return
        k[cname] = n
        sem, val = self.counters[cname].sem_for(n)
        self.prog[eng].append(("wait", sem, val))

    def _deps(self, eng, reads, writes, waw=True):
        for v in reads:
            for d in v.buf.w.items():
                self._need(eng, d)
        for v in writes:
            if waw:
                for d in v.buf.w.items():
                    self._need(eng, d)
            for r in v.buf.r:
                self._need(eng, r)

    def _commit(self, tag, reads, writes):
        for v in reads:
            v.buf.r.append(tag)
            if len(v.buf.r) > 64:
                d = {}
                for c, n in v.buf.r:
                    d[c] = max(d.get(c, 0), n)
                v.buf.r = list(d.items())
        for v in writes:
            v.buf.w[tag[0]] = tag[1]
            v.buf.r = []

    def op(self, eng, name, *args, **kw):
        reads, writes = [], []
        for k, a in kw.items():
            if isinstance(a, View):
                (writes if k in WRITE_KEYS else reads).append(a)
        for a in args:
            assert not isinstance(a, View), "use kwargs for views"
        extra_r = kw.pop("_reads", [])
        extra_w = kw.pop("_writes", [])
        reads += extra_r
        writes += extra_w
        self._deps(eng, reads, writes)
        c = self.cnt[eng]
        c.n += 1
        n = c.n
        sem, _ = c.sem_for(n)
        rk = {k: (a.ap if isinstance(a, View) else a) for k, a in kw.items()}
        self.prog[eng].append(("op", name, args, rk, sem, 1))
        self._commit((eng, n), reads, writes)
        return (eng, n)

    def dma(self, q, out, in_, waw=True, **kw):
        slots = self.dma_slots[q]
        i = self.dma_rr[q]
        self.dma_rr[q] = (i + 1) % len(slots)
        c = slots[i]
        self._need(q, (c.name, c.n))
        self._deps(q, [in_], [out], waw=waw)
        c.n += 1
        sem, _ = c.sem_for(c.n)
        self.prog[q].append(("dma", out.ap, in_.ap, kw, sem))
        tag = (c.name, c.n)
        self._commit(tag, [in_], [out])
        return tag

    def cc_allgather(self, out, in_, groups):
        q = "pool"
        slots = self.dma_slots[q]
        i = self.dma_rr[q]
        self.dma_rr[q] = (i + 1) % len(slots)
        c = slots[i]
        self._need(q, (c.name, c.n))
        self._deps(q, [in_], [out])
        c.n += 1
        sem, _ = c.sem_for(c.n)
        self.prog[q].append(("cc", out.ap, in_.ap, groups, sem))
        tag = (c.name, c.n)
        self._commit(tag, [in_], [out])
        return tag

    def barrier(self):
        for e in ENGS:
            for cname, c in self.counters.items():
                if cname == e and e == "pe":
                    continue
                self._need(e, (cname, c.n))

    def wait_all(self, eng):
        for cname, c in self.counters.items():
            self._need(eng, (cname, c.n))

    def emit(self, block):
        nc = self.nc
        S = self

        def run(engname, eng):
            for item in S.prog[engname]:
                if item[0] == "wait":
                    eng.wait_ge(item[1], item[2])
                elif item[0] == "op":
                    _, name, args, kw, sem, inc = item
                    ins = getattr(eng, name)(*args, **kw)
                    ins.then_inc(sem, inc)
                elif item[0] == "dma":
                    _, o, i, kw, sem = item
                    eng.dma_start(out=o, in_=i, **kw).then_inc(sem, 16)
                elif item[0] == "cc":
                    _, o, i, groups, sem = item
                    eng.collective_compute("AllGather", op=ALU.bypass, replica_groups=groups, ins=[i], outs=[o]).then_inc(sem, 16)

        @block.tensor
        def _(e):
            run("pe", e)

        @block.scalar
        def _(e):
            run("act", e)

        @block.vector
        def _(e):
            run("dve", e)

        @block.gpsimd
        def _(e):
            run("pool", e)

        @block.sync
        def _(e):
            run("sp", e)


from contextlib import ExitStack
from concourse.bass_utils import run_bass_kernel_spmd

D = 1024
SEQ = 4096
L = 2
EPS = 1e-6
NBLK = 8
PBT = 256
C_Q, C_K, C_V, C_SB, C_SC, C_SH, C_GQ, C_GK, C_GV, C_GR, C_LR, C_QS, C_KS = (
    0, 512, 1024, 1536, 1792, 2048, 2304, 2432, 2560, 2816, 3072, 3088, 3600)
INW = 4112
PI = float(np.pi)


class Prog:
    def __init__(self, n_layers=L, do_peer=True, dbg=(), stage=9, nblk=NBLK, peer_test=0):
        self.stage = stage
        self.nblk = nblk
        self.n_layers = n_layers
        self.do_peer = do_peer
        self.dbg = dbg
        nc = self.nc = bass.Bass("TRN2", target_bir_lowering=False)
        S = self.S = Sched(nc)
        self.din = {}
        self.dout = {}

        def din(name, shape, dt=F32):
            self.din[name] = Buf(S, nc.dram_tensor(name, list(shape), dt, kind="ExternalInput").ap(), name)
            return self.din[name]

        din("x", [SEQ, D]); din("pos", [1, SEQ], I32)
        din("norm_mix", [L, D]); din("norm_ffn", [L, D]); din("norm_final", [1, D])
        din("w_in", [L, D, INW]); din("lam", [L, 256]); din("diff_norm", [L, 128])
        din("conv_w", [L, 128, 6]); din("gate_aug", [L, 17, 128]); din("gla_norm", [L, 64])
        din("w_out", [L, D, D]); din("w_q", [L, D, 2048]); din("k1T", [L, 128, 128]); din("k2T", [L, 128, 128])
        din("uT", [L, 128, 128, 1024]); din("vr", [L, 128, 128, 1024]); din("rope_c", [128, 2])
        self.split_last = bool(do_peer and not peer_test and n_layers == L)
        din("sel", [1, 2])
        self.out = Buf(S, nc.dram_tensor("out", [SEQ // 2 if self.split_last else SEQ, D], F32, kind="ExternalOutput").ap(), "out")
        self.xs = Buf(S, nc.dram_tensor("xs", [SEQ, D], F32, kind="Internal").ap(), "xs")
        self.uTb = Buf(S, nc.dram_tensor("uTb", [128, 128, 1024], BF16, kind="Internal").ap(), "uTb")
        self.vrb = Buf(S, nc.dram_tensor("vrb", [128, 128, 1024], BF16, kind="Internal").ap(), "vrb")
        for name, shape in dbg:
            self.dout[name] = Buf(S, nc.dram_tensor(name, list(shape), F32, kind="ExternalOutput").ap(), name)

        with ExitStack() as top:
            self.top = top
            self.PB = [Buf(S, top.enter_context(nc.psum_tensor(f"pb{i}", [128, 512], F32)), f"pb{i}") for i in range(8)]
            self.consts()
            if peer_test:
                with ExitStack() as es2:
                    cp = [self.sb(es2, f"cq{i}", [128, D]) for i in range(2)]
                    for i in range(SEQ // 128):
                        S.dma("sp", cp[i % 2][:, :], self.din["x"][i * 128:(i + 1) * 128, :])
                        S.dma("sp", self.xs[i * 128:(i + 1) * 128, :], cp[i % 2][:, :])
                    S.barrier()
                self.convert_weights(0, 0, 1)
                self.peer(0, last=False, nblk=peer_test)
                S.barrier()
                n_layers = 0
                do_peer = False
            for l in range(n_layers):
                src = self.din["x"] if l == 0 else self.xs
                self.mixer(l, src)
                S.barrier()
                if do_peer:
                    self.peer(l, last=(l == n_layers - 1))
                    S.barrier()
            if not do_peer:
                with ExitStack() as es2:
                    cp = [self.sb(es2, f"cp{i}", [128, D]) for i in range(2)]
                    for i in range(SEQ // 128):
                        S.dma("sp", cp[i % 2][:, :], self.xs[i * 128:(i + 1) * 128, :])
                        S.dma("sp", self.out[i * 128:(i + 1) * 128, :], cp[i % 2][:, :])
                    S.wait_all("sp")
            S.wait_all("sp")
            S.wait_all("pool")
            with nc.Block() as block:
                S.emit(block)

    def sb(self, es, name, shape, dt=F32):
        return Buf(self.S, es.enter_context(self.nc.sbuf_tensor(name, list(shape), dt)), name)

    def pbf(self, i, n=1024):
        b = self.PB[i]
        return View(b, b.tensor[:, :].bitcast(BF16))[:, 0:n]

    def tap(self, name, view):
        if name in self.dout:
            self.S.dma("sp", self.dout[name].v(self.dout[name].tensor), view)

    def consts(self):
        S, es = self.S, self.top
        self.idf = self.sb(es, "idf", [128, 128]); self.idb = self.sb(es, "idb", [128, 128], BF16)
        S.op("pool", "memset", ap=self.idf[:, :], constant=0.0)
        S.op("pool", "affine_select", out=self.idf[:, :], in_=self.idf[:, :], pattern=[[-1, 128]],
             compare_op=ALU.not_equal, fill=1.0, base=0, channel_multiplier=1)
        S.op("dve", "tensor_copy", out=self.idb[:, :], in_=self.idf[:, :])
        self.iota = self.sb(es, "iota", [128, 128])
        S.op("pool", "iota", out=self.iota[:, :], pattern=[[1, 128]], base=0, channel_multiplier=0,
             allow_small_or_imprecise_dtypes=True)
        self.ropec = self.sb(es, "ropec", [128, 2])
        S.dma("sp", self.ropec[:, :], self.din["rope_c"][:, :])

    def convert_weights(self, l, part, nparts):
        S, din = self.S, self.din
        if nparts == NBLK:
            cuts = [0, 2, 6, 11, 18, 27, 38, 50, 64]
            lo_, hi_ = cuts[part], cuts[part + 1]
        else:
            n = 64 // nparts
            lo_, hi_ = part * n, (part + 1) * n
        for i in range(lo_, hi_):
            for srcn, dst in (("uT", self.uTb), ("vr", self.vrb)):
                dv = dst.tensor[:, :, :].rearrange("a b (c d) -> (a b c) d", d=4096) if False else dst.tensor[:, :, :].rearrange("a (b2 b4) c -> (a b2) (b4 c)", b4=4)
                sv = din[srcn].tensor[l].rearrange("a (b2 b4) c -> (a b2) (b4 c)", b4=4)
                S.dma("pool", dst.v(dv[64 * i:64 * i + 64, :]), din[srcn].v(sv[64 * i:64 * i + 64, :]), waw=(i == 0))

    def rmsnorm_tile(self, x, gbc, hout, sq, ss, width=D):
        S = self.S
        S.op("act", "activation", out=sq, in_=x, func=AF.Square, accum_out=ss[:, 0:1])
        S.op("act", "activation", out=ss[:, 1:2], in_=ss[:, 0:1], func=AF.Sqrt, scale=1.0 / width, bias=self.epsb[:, 0:1])
        S.op("dve", "reciprocal", out=ss[:, 1:2], in_=ss[:, 1:2])
        S.op("dve", "scalar_tensor_tensor", out=hout, in0=x, scalar=ss[:, 1:2], in1=gbc, op0=ALU.mult, op1=ALU.mult)

    def mixer(self, l, src):
        S, nc, PB = self.S, self.nc, self.PB
        din = self.din
        lam_init = 0.8 - 0.6 * float(np.exp(-0.3 * l))
        with ExitStack() as es:
            sb = lambda name, shape, dt=F32: self.sb(es, f"m{l}_{name}", shape, dt)
            self.epsb = sb("epsb", [128, 1]); S.op("pool", "memset", ap=self.epsb[:, :], constant=EPS)
            wbuf = [sb(f"wbuf{i}", [128, 8, 1024], BF16) for i in range(2)]
            wout = sb("wout", [128, 8, D], BF16)
            KT = sb("KT", [128, 4, SEQ], BF16)
            VA = sb("VA", [128, 32, 4, 130], BF16)
            gbc = sb("gbc", [128, D]); gnb = sb("gnb", [128, 64])
            lams = sb("lams", [128, 8])
            cw = sb("cw", [128, 6]); gate = sb("gate", [17, 128])
            xt = [sb(f"xt{i}", [128, D]) for i in range(2)]
            htm = sb("htm", [128, D], BF16); sq = htm; ss = sb("ss", [128, 2])
            hT = sb("hT", [128, 8, 512], BF16)
            QT = sb("QT", [128, 4, 2, 512], BF16)
            S.op("pool", "memset", ap=QT[:, :, :, :], constant=0.0)
            mixT = sb("mixT", [128, 8, 512], BF16)
            posi = sb("posi", [128, 512], I32); ang = sb("ang", [128, 512]); ang2 = sb("ang2", [128, 512])
            angi = posi; wk = sb("wk", [128, 512])
            Ct = sb("Ct", [128, 512]); St = sb("St", [128, 512])
            t1 = sb("t1", [128, 512]); t2 = sb("t2", [128, 512]); lamt = t2
            zb = sb("zb", [128, 2, 514]); csb = sb("csb", [128, 512]); yb = sb("yb", [128, 512])
            gqT = sb("gqT", [128, 512]); gkT = sb("gkT", [128, 512]); glr = sb("glr", [17, 512])
            tri01 = sb("tri01", [64, 4, 64], BF16); triS = sb("triS", [64, 64]); bd = sb("bd", [128, 4, 64])
            ex = sb("ex", [64, 128]); sp = sb("sp", [64, 128])
            eb = [sb(f"eb{i}", [128, 64]) for i in range(2)]; enb = sb("enb", [128, 64])
            qtl = [sb(f"qtl{i}", [128, 64], BF16) for i in range(2)]; ktl = sb("ktl", [128, 64], BF16)
            qbd = sb("qbd", [128, 4, 64], BF16); attnT = [sb(f"attnT{i}", [64, 4, 64], BF16) for i in range(2)]
            ktm = [sb(f"ktm{i}", [64, 128], BF16) for i in range(2)]; gv = [sb(f"gv{i}", [64, 256], BF16) for i in range(2)]
            sil = [sb(f"sil{i}", [64, 256]) for i in range(2)]
            S32 = sb("S32", [128, 256]); Sbd = sb("Sbd", [128, 256], BF16); tkv = t1[:, 0:256]
            osq = sb("osq", [64, 256]); oss = sb("oss", [64, 8]); og = osq; ogb = sb("ogb", [64, 256], BF16)
            PT = [sb(f"PT{i}", [128, 512], BF16) for i in range(4)]
            accs = [sb(f"acc{i}", [128, 512]) for i in range(2)]
            onesf = sb("onesf", [128, 128]); dnT = sb("dnT", [128, 1])
            S.op("pool", "memset", ap=onesf[:, :], constant=1.0)

            S.dma("pool", wout[:, :, :], din["w_out"].v(din["w_out"].tensor[l].rearrange("(c p) n -> p c n", p=128)))
            S.dma("sp", gbc[:, :], din["norm_mix"].v(din["norm_mix"].tensor[l:l + 1, :].partition_broadcast(128)))
            S.dma("sp", gnb[:, :], din["gla_norm"].v(din["gla_norm"].tensor[l:l + 1, :].partition_broadcast(128)))
            S.dma("sp", lamt[:, 0:256], din["lam"].v(din["lam"].tensor[l:l + 1, :].partition_broadcast(128)))
            S.dma("sp", cw[:, :], din["conv_w"].v(din["conv_w"].tensor[l]))
            S.dma("sp", gate[:, :], din["gate_aug"].v(din["gate_aug"].tensor[l]))
            S.dma("sp", dnT[:, :], din["diff_norm"].v(din["diff_norm"].tensor[l, :].rearrange("(p o) -> p o", o=1)))
            S.op("dve", "tensor_scalar", out=dnT[:, :], in0=dnT[:, :], scalar1=1.0 - lam_init, scalar2=None, op0=ALU.mult)
            S.op("dve", "tensor_tensor", out=lamt[:, 0:64], in0=lamt[:, 0:64], in1=lamt[:, 64:128], op=ALU.mult)
            S.op("dve", "tensor_tensor", out=lamt[:, 128:192], in0=lamt[:, 128:192], in1=lamt[:, 192:256], op=ALU.mult)
            S.op("dve", "reduce_sum", out=lams[:, 0:1], in_=lamt[:, 0:64], axis=AX.X)
            S.op("dve", "reduce_sum", out=lams[:, 1:2], in_=lamt[:, 128:192], axis=AX.X)
            S.op("act", "activation", out=lams[:, 2:4], in_=lams[:, 0:2], func=AF.Exp)
            S.op("dve", "tensor_tensor", out=lams[:, 4:5], in0=lams[:, 3:4], in1=lams[:, 2:3], op=ALU.subtract)
            S.op("dve", "tensor_scalar", out=lams[:, 5:6], in0=lams[:, 4:5], scalar1=-lam_init, scalar2=None, op0=ALU.add)
            nlam = lams[:, 5:6]
            S.op("pool", "memset", ap=tri01[:, :, :], constant=1.0)
            S.op("pool", "affine_select", out=tri01[:, :, :], in_=tri01[:, :, :], pattern=[[0, 4], [1, 64]],
                 compare_op=ALU.is_ge, fill=0.0, base=0, channel_multiplier=-1)
            S.op("pool", "memset", ap=triS[:, :], constant=-1.0 / 16.0)
            S.op("pool", "affine_select", out=triS[:, :], in_=triS[:, :], pattern=[[1, 64]],
                 compare_op=ALU.is_ge, fill=0.0, base=0, channel_multiplier=-1)
            S.op("pool", "memset", ap=bd[:, :, :], constant=0.0)
            for h in range(4):
                S.op("pool", "memset", ap=bd[32 * h:32 * h + 32, h, :], constant=1.0)
            S.op("pool", "memset", ap=VA[:, :, :, 128:130], constant=1.0)
            S.op("pool", "memset", ap=glr[:, :], constant=1.0)
            S.op("pool", "memset", ap=zb[:, :, :], constant=0.0)
            S.op("pool", "memset", ap=S32[:, :], constant=0.0)
            S.op("pool", "memset", ap=Sbd[:, :], constant=0.0)

            W_IN = din["w_in"]
            wsrc = W_IN.tensor[l].rearrange("(c p) n -> p c n", p=128)
            wstate = {"n": 0}

            def load_w(cols):
                b = wbuf[wstate["n"] % 2]
                wstate["n"] += 1
                offs = []
                o = 0
                for ci_, (c0, n) in enumerate(cols):
                    S.dma("pool", b[:, :, o:o + n], W_IN.v(wsrc[:, :, c0:c0 + n]), waw=(ci_ == 0))
                    offs.append(o)
                    o += n
                return b, offs

            def proj_fm(bank, wb, off, m, n=512):
                for c in range(8):
                    S.op("pe", "matmul", out=PB[bank][0:m, 0:n], lhsT=wb[:, c, off:off + m], rhs=hT[:, c, 0:n],
                         start=(c == 0), stop=(c == 7))

            def reduce_angle(a):
                S.op("dve", "tensor_scalar", out=angi[:, :], in0=a[:, :], scalar1=1.0 / (2 * PI), scalar2=None, op0=ALU.mult)
                S.op("dve", "tensor_copy", out=wk[:, :], in_=angi[:, :])
                S.op("dve", "scalar_tensor_tensor", out=a[:, :], in0=wk[:, :], scalar=-2 * PI, in1=a[:, :], op0=ALU.mult, op1=ALU.add)
                S.op("dve", "tensor_scalar", out=wk[:, :], in0=a[:, :], scalar1=PI, scalar2=-2 * PI, op0=ALU.is_gt, op1=ALU.mult)
                S.op("dve", "tensor_tensor", out=a[:, :], in0=a[:, :], in1=wk[:, :], op=ALU.add)
                S.op("dve", "tensor_scalar", out=wk[:, :], in0=a[:, :], scalar1=-PI, scalar2=2 * PI, op0=ALU.is_lt, op1=ALU.mult)
                S.op("dve", "tensor_tensor", out=a[:, :], in0=a[:, :], in1=wk[:, :], op=ALU.add)

            XS = src
            for jb in range(self.nblk):
                t0 = jb * 512
                for tt in range(4):
                    xb_ = xt[tt % 2]
                    S.dma("sp", xb_[:, :], XS[t0 + tt * 128:t0 + (tt + 1) * 128, :])
                    self.rmsnorm_tile(xb_[:, :], gbc[:, :], htm[:, :], sq[:, :], ss)
                    bank = tt % 2
                    pv = self.pbf(bank)
                    for c in range(8):
                        S.op("pe", "transpose", out=pv[:, c * 128:(c + 1) * 128], in_=htm[:, c * 128:(c + 1) * 128], identity=self.idb[:, :])
                    S.op("act", "activation", out=hT[:, :, tt * 128:(tt + 1) * 128],
                         in_=pv.rearrange("p (c t) -> p c t", t=128), func=AF.Copy)
                if self.stage < 2:
                    continue
                S.dma("sp", posi[:, :], din["pos"].v(din["pos"].tensor[0:1, t0:t0 + 512].partition_broadcast(128)))
                S.op("dve", "tensor_copy", out=ang[:, :], in_=posi[:, :])
                S.op("dve", "tensor_scalar", out=ang[:, :], in0=ang[:, :], scalar1=self.ropec[:, 0:1], scalar2=None, op0=ALU.mult)
                S.op("dve", "tensor_scalar", out=ang2[:, :], in0=ang[:, :], scalar1=PI / 2, scalar2=None, op0=ALU.add)
                reduce_angle(ang)
                reduce_angle(ang2)
                S.op("act", "activation", out=St[:, :], in_=ang[:, :], func=AF.Sin)
                S.op("act", "activation", out=Ct[:, :], in_=ang2[:, :], func=AF.Sin)
                S.op("dve", "tensor_scalar", out=St[:, :], in0=St[:, :], scalar1=self.ropec[:, 1:2], scalar2=None, op0=ALU.mult)
                for which, (c_a, c_s) in enumerate(((C_Q, C_QS), (C_K, C_KS))):
                    wb, offs = load_w([(c_a, 512), (c_s, 512)])
                    for i in range(4):
                        ba, bb = (2 * i) % 8, (2 * i + 1) % 8
                        proj_fm(ba, wb, offs[0] + i * 128, 128)
                        proj_fm(bb, wb, offs[1] + i * 128, 128)
                        S.op("dve", "tensor_tensor", out=t1[:, :], in0=PB[ba][:, :], in1=Ct[:, :], op=ALU.mult)
                        S.op("dve", "tensor_tensor", out=t2[:, :], in0=PB[bb][:, :], in1=St[:, :], op=ALU.mult)
                        if which == 0:
                            S.op("pool", "tensor_tensor", out=QT[0:64, i, 0, :], in0=t1[0:64, :], in1=t2[0:64, :], op=ALU.add)
                            S.op("pool", "tensor_tensor", out=QT[64:128, i, 1, :], in0=t1[64:128, :], in1=t2[64:128, :], op=ALU.add)
                        else:
                            S.op("pool", "tensor_tensor", out=KT[:, i, t0:t0 + 512], in0=t1[:, :], in1=t2[:, :], op=ALU.add)
                if self.stage < 3:
                    continue
                wb, offs = load_w([(C_SB, 768)])
                for ch in range(2):
                    proj_fm(0, wb, 256 + ch * 128, 128)
                    proj_fm(1, wb, 512 + ch * 128, 128)
                    proj_fm(2, wb, 0 + ch * 128, 128)
                    S.op("act", "activation", out=csb[:, :], in_=PB[0][:, :], func=AF.Copy)
                    S.op("dve", "tensor_tensor", out=zb[:, ch, 2:514], in0=csb[:, :], in1=PB[1][:, :], op=ALU.mult)
                    S.op("dve", "tensor_scalar", out=yb[:, :], in0=zb[:, ch, 0:512], scalar1=cw[:, ch * 3 + 0:ch * 3 + 1], scalar2=None, op0=ALU.mult)
                    S.op("dve", "scalar_tensor_tensor", out=yb[:, :], in0=zb[:, ch, 1:513], scalar=cw[:, ch * 3 + 1:ch * 3 + 2], in1=yb[:, :], op0=ALU.mult, op1=ALU.add)
                    S.op("dve", "scalar_tensor_tensor", out=yb[:, :], in0=zb[:, ch, 2:514], scalar=cw[:, ch * 3 + 2:ch * 3 + 3], in1=yb[:, :], op0=ALU.mult, op1=ALU.add)
                    S.op("dve", "tensor_tensor", out=mixT[:, 4 + ch, :], in0=PB[2][:, :], in1=yb[:, :], op=ALU.mult)
                    S.op("dve", "tensor_copy", out=zb[:, ch, 0:2], in_=zb[:, ch, 512:514])
                wb, offs = load_w([(C_GQ, 256), (C_LR, 16), (C_V, 512)])
                proj_fm(3, wb, 0, 128); S.op("act", "activation", out=gqT[:, :], in_=PB[3][:, :], func=AF.Copy)
                proj_fm(4, wb, 128, 128); S.op("act", "activation", out=gkT[:, :], in_=PB[4][:, :], func=AF.Copy)
                proj_fm(5, wb, 256, 16); S.op("act", "activation", out=glr[0:16, :], in_=PB[5][0:16, :], func=AF.Copy)
                for tt in range(4):
                    bank = 6 + tt % 2
                    for c in range(8):
                        S.op("pe", "matmul", out=PB[bank][:, :], lhsT=hT[:, c, tt * 128:(tt + 1) * 128], rhs=wb[:, c, 272:784],
                             start=(c == 0), stop=(c == 7))
                    S.op("act", "activation", out=VA[:, jb * 4 + tt, :, 0:128],
                         in_=PB[bank][:, :].rearrange("p (h e) -> p h e", e=128), func=AF.Copy)
                wb, offs = load_w([(C_GV, 512)])
                if self.stage < 4:
                    continue
                def gla_s1(ck):
                    tc = ck * 64
                    k = ck % 2
                    S.op("pe", "matmul", out=PB[0][0:64, 0:128], lhsT=glr[0:17, tc:tc + 64], rhs=gate[0:17, :], start=True, stop=True)
                    yield
                    S.op("act", "activation", out=ex[:, :], in_=PB[0][0:64, 0:128], func=AF.Exp, scale=-1.0)
                    yield
                    S.op("act", "activation", out=sp[:, :], in_=ex[:, :], func=AF.Ln, bias=1.0)
                    yield
                    S.op("pe", "matmul", out=PB[1][:, 0:64], lhsT=sp[:, :], rhs=triS[:, :], start=True, stop=True)
                    yield
                    S.op("act", "activation", out=eb[k][:, :], in_=PB[1][:, 0:64], func=AF.Exp)
                    S.op("act", "activation", out=enb[:, :], in_=PB[1][:, 0:64], func=AF.Exp, scale=-1.0)
                    yield
                    S.op("dve", "scalar_tensor_tensor", out=qtl[k][:, :], in0=gqT[:, tc:tc + 64], scalar=32 ** -0.5, in1=eb[k][:, :], op0=ALU.mult, op1=ALU.mult)
                    S.op("dve", "tensor_tensor", out=ktl[:, :], in0=gkT[:, tc:tc + 64], in1=enb[:, :], op=ALU.mult)
                    yield
                    S.op("pool", "tensor_tensor", out=qbd[:, :, :], in0=View(qtl[k], qtl[k].tensor[:, :].unsqueeze(1).to_broadcast([128, 4, 64])),
                         in1=bd[:, :, :], op=ALU.mult)
                    pv3 = self.pbf(3)
                    S.op("pe", "transpose", out=pv3[0:64, 0:128], in_=ktl[:, :], identity=self.idb[:, :])
                    yield
                    S.op("act", "activation", out=ktm[k][:, :], in_=pv3[0:64, 0:128], func=AF.Copy)
                    S.op("pe", "matmul", out=PB[2][0:64, 0:256], lhsT=ktl[:, :], rhs=qbd[:, :, :].rearrange("p h i -> p (h i)"), start=True, stop=True)
                    yield
                    S.op("dve", "tensor_tensor", out=attnT[k][:, :, :], in0=PB[2][0:64, 0:256].rearrange("p (h i) -> p h i", i=64), in1=tri01[:, :, :], op=ALU.mult)
                    for c in range(8):
                        S.op("pe", "matmul", out=PB[4][0:64, :], lhsT=hT[:, c, tc:tc + 64], rhs=wb[:, c, 0:512], start=(c == 0), stop=(c == 7))
                    yield
                    S.op("act", "activation", out=gv[k][:, :], in_=PB[4][0:64, 0:256], func=AF.Copy)
                    yield
                    S.op("act", "activation", out=sil[k][:, :], in_=PB[4][0:64, 256:512], func=AF.Silu)
                    yield

                def gla_s2(ck):
                    tc = ck * 64
                    k = ck % 2
                    S.op("pe", "matmul", out=PB[5][0:64, 0:256], lhsT=qtl[k][:, :], rhs=Sbd[:, :], start=True, stop=True)
                    for h in range(4):
                        S.op("pe", "matmul", out=PB[5][0:64, h * 64:(h + 1) * 64], lhsT=attnT[k][:, h, :], rhs=gv[k][:, h * 64:(h + 1) * 64],
                             start=False, stop=(h == 3), skip_group_check=True)
                    S.op("pe", "matmul", out=PB[6][:, 0:256], lhsT=ktm[k][:, :], rhs=gv[k][:, :], start=True, stop=True)
                    yield
                    S.op("dve", "scalar_tensor_tensor", out=tkv, in0=PB[6][:, 0:256], scalar=eb[k][:, 63:64],
                         in1=bd[:, :, :].rearrange("p h e -> p (h e)"), op0=ALU.mult, op1=ALU.mult)
                    S.op("act", "activation", out=osq[:, :], in_=PB[5][0:64, 0:256], func=AF.Square)
                    yield
                    S.op("dve", "scalar_tensor_tensor", out=S32[:, :], in0=S32[:, :], scalar=eb[k][:, 63:64], in1=tkv, op0=ALU.mult, op1=ALU.add)
                    yield
                    S.op("pool", "tensor_copy", out=Sbd[:, :], in_=S32[:, :])
                    S.op("dve", "reduce_sum", out=oss[:, 0:4], in_=osq[:, :].rearrange("p (h e) -> p h e", e=64), axis=AX.X)
                    yield
                    S.op("act", "activation", out=oss[:, 4:8], in_=oss[:, 0:4], func=AF.Sqrt, scale=1.0 / 64, bias=self.epsb[0:64, 0:1])
                    yield
                    S.op("dve", "reciprocal", out=oss[:, 4:8], in_=oss[:, 4:8])
                    yield
                    S.op("dve", "tensor_tensor", out=og[:, :].rearrange("p (h e) -> p h e", e=64),
                         in0=PB[5][0:64, 0:256].rearrange("p (h e) -> p h e", e=64),
                         in1=View(oss, oss.tensor[0:64, 4:8].unsqueeze(2).to_broadcast([64, 4, 64])), op=ALU.mult)
                    yield
                    S.op("pool", "tensor_tensor", out=og[:, :].rearrange("p (h e) -> p h e", e=64),
                         in0=og[:, :].rearrange("p (h e) -> p h e", e=64),
                         in1=View(gnb, gnb.tensor[0:64, :].unsqueeze(1).to_broadcast([64, 4, 64])), op=ALU.mult)
                    yield
                    S.op("dve", "tensor_tensor", out=ogb[:, :], in0=og[:, :], in1=sil[k][:, :], op=ALU.mult)
                    yield
                    pv7 = self.pbf(7)
                    for hh in range(2):
                        S.op("pe", "transpose", out=pv7[:, hh * 64:(hh + 1) * 64], in_=ogb[:, hh * 128:(hh + 1) * 128], identity=self.idb[0:64, 0:64])
                    yield
                    S.op("act", "activation", out=mixT[:, 6:8, tc:tc + 64], in_=pv7[:, 0:128].rearrange("p (h t) -> p h t", t=64), func=AF.Copy)
                    yield

                for ck in range(-1, 8):
                    g1 = gla_s1(ck + 1) if ck + 1 < 8 else iter(())
                    g2 = gla_s2(ck) if ck >= 0 else iter(())
                    d1 = d2 = False
                    while not (d1 and d2):
                        if not d1:
                            d1 = next(g1, "done") == "done"
                        if not d2:
                            d2 = next(g2, "done") == "done"
                if self.stage < 5:
                    continue
                if self.do_peer:
                    self.convert_weights(l, jb, NBLK)
                nkt = 4 * jb + 4
                items = [(h, m, kt) for h in range(4) for m in range(2) for kt in range(nkt)]

                def obank(h, m):
                    return 3 + 2 * (h % 2) + m

                def st1(i):
                    h, m, kt = items[i]
                    pr = slice(64 * m, 64 * m + 64)
                    o = kt - 4 * jb
                    q0 = max(o, 0) * 128
                    S.op("pe", "matmul", out=PB[i % 3][:, q0:512], lhsT=KT[:, h, kt * 128:(kt + 1) * 128], rhs=QT[:, h, m, q0:512],
                         start=True, stop=True)
                    p = PT[i % 4]
                    S.op("act", "activation", out=p[:, q0:512], in_=PB[i % 3][:, q0:512], func=AF.Exp, scale=0.125)
                    if o >= 0:
                        S.op("pool", "affine_select", out=p[:, q0:q0 + 128], in_=p[:, q0:q0 + 128], pattern=[[1, 128]],
                             compare_op=ALU.is_ge, fill=0.0, base=0, channel_multiplier=-1)

                def st2(i):
                    h, m, kt = items[i]
                    o = kt - 4 * jb
                    q0 = max(o, 0) * 128
                    p = PT[i % 4]
                    acc = accs[m]
                    S.op("pe", "matmul", out=PB[obank(h, m)][:, q0:512], lhsT=VA[:, kt, h, 0:128], rhs=p[:, q0:512],
                         start=(kt == 0), stop=(kt == nkt - 1))
                    accB = (gqT, gkT)[m]
                    if kt % 2 == 0:
                        if kt == 0:
                            S.op("dve", "tensor_copy", out=acc[:, :], in_=p[:, :])
                        else:
                            S.op("dve", "tensor_tensor", out=acc[:, q0:512], in0=acc[:, q0:512], in1=p[:, q0:512], op=ALU.add)
                    else:
                        if kt == 1:
                            if q0 > 0:
                                S.op("dve", "memset", ap=accB[:, 0:q0], constant=0.0)
                            S.op("dve", "tensor_copy", out=accB[:, q0:512], in_=p[:, q0:512])
                        else:
                            S.op("dve", "tensor_tensor", out=accB[:, q0:512], in0=accB[:, q0:512], in1=p[:, q0:512], op=ALU.add)
                    if kt == nkt - 1:
                        rinv = t1 if m == 0 else t2
                        S.op("pe", "matmul", out=PB[7][:, :], lhsT=onesf[:, :], rhs=acc[:, :], start=True, stop=False)
                        S.op("pe", "matmul", out=PB[7][:, :], lhsT=onesf[:, :], rhs=accB[:, :], start=False, stop=True)
                        S.op("act", "activation", out=rinv[:, :], in_=PB[7][:, :], func=AF.Ln)
                        S.op("act", "activation", out=rinv[:, :], in_=rinv[:, :], func=AF.Exp, scale=-1.0)
                        if m == 1:
                            epilogue(h)

                def epilogue(h):
                    S.op("dve", "tensor_tensor", out=csb[:, :], in0=PB[obank(h, 0)][:, :], in1=t1[:, :], op=ALU.mult)
                    S.op("dve", "tensor_tensor", out=yb[:, :], in0=PB[obank(h, 1)][:, :], in1=t2[:, :], op=ALU.mult)
                    S.op("dve", "scalar_tensor_tensor", out=yb[:, :], in0=yb[:, :], scalar=nlam, in1=csb[:, :], op0=ALU.mult, op1=ALU.add)
                    S.op("act", "activation", out=wk[:, :], in_=yb[:, :], func=AF.Square)
                    S.op("pe", "matmul", out=PB[7][:, :], lhsT=onesf[:, :], rhs=wk[:, :], start=True, stop=True)
                    S.op("act", "activation", out=ang[:, :], in_=PB[7][:, :], func=AF.Ln, scale=1.0 / 128, bias=self.epsb[:, 0:1])
                    S.op("act", "activation", out=ang[:, :], in_=ang[:, :], func=AF.Exp, scale=-0.5)
                    S.op("dve", "scalar_tensor_tensor", out=mixT[:, h, :], in0=yb[:, :], scalar=dnT[:, 0:1], in1=ang[:, :], op0=ALU.mult, op1=ALU.mult)

                for i in range(len(items) + 2):
                    if i < len(items):
                        st1(i)
                    if i >= 2:
                        st2(i - 2)
                if "mixT" in self.dout and jb == 0:
                    pass
                for tt in range(4):
                    xb_ = xt[tt % 2]
                    S.dma("sp", xb_[:, :], XS[t0 + tt * 128:t0 + (tt + 1) * 128, :])
                    for half in range(2):
                        bank = 2 + (tt % 2) * 2 + half
                        for c in range(8):
                            S.op("pe", "matmul", out=PB[bank][:, :], lhsT=mixT[:, c, tt * 128:(tt + 1) * 128], rhs=wout[:, c, half * 512:(half + 1) * 512],
                                 start=(c == 0), stop=(c == 7))
                        S.op("dve", "tensor_tensor", out=xb_[:, half * 512:(half + 1) * 512], in0=PB[bank][:, :], in1=xb_[:, half * 512:(half + 1) * 512], op=ALU.add)
                    S.dma("sp", self.xs[t0 + tt * 128:t0 + (tt + 1) * 128, :], xb_[:, :])

    def peer(self, l, last, nblk=None):
        S, nc, PB = self.S, self.nc, self.PB
        din = self.din
        GE = 2
        split = bool(last and self.split_last)
        nblk = ((SEQ // PBT) // (2 if split else 1)) if nblk is None else nblk
        with ExitStack() as es:
            sb = lambda name, shape, dt=F32: self.sb(es, f"p{l}_{name}", shape, dt)
            self.epsb = sb("epsb", [128, 1]); S.op("pool", "memset", ap=self.epsb[:, :], constant=EPS)
            wq = sb("wq", [128, 8, 2048], BF16)
            k1T = sb("k1T", [128, 128], BF16); k2T = sb("k2T", [128, 128], BF16)
            nfb = sb("nfb", [128, D]); nfin = sb("nfin", [128, D])
            G = sb("G", [128, 128, PBT], BF16)
            uTs = [sb(f"uTs{i}", [128, GE, 1024], BF16) for i in range(2)]
            vs = [sb(f"vs{i}", [128, GE, 1024], BF16) for i in range(2)]
            xp = [sb(f"xp{i}", [128, D]) for i in range(4)]
            if split:
                xalt = sb("xalt", [128, D]); selb = sb("selb", [128, 2])
                S.dma("sp", selb[:, :], din["sel"].v(din["sel"].tensor[0:1, :].partition_broadcast(128)))
            xn = sb("xn", [128, D], BF16); sq = sb("sq", [128, D], BF16); ss = sb("ss", [128, 2])
            xnT = [sb(f"xnT{i}", [128, 8, PBT], BF16) for i in range(2)]
            qT = sb("qT", [128, 16, PBT], BF16)
            sc = sb("sc", [128, 16, 128])
            yo = View(sc, sc.tensor[:, 0:8, :].rearrange("p a b -> p (a b)"))
            wk1 = sb("wk1", [128, 128]); wk2 = sb("wk2", [128, 128]); wk3 = sb("wk3", [128, 128])
            v1 = sb("v1", [128, 8, 16]); idx1 = sb("idx1", [128, 8, 16], U32); v2 = sb("v2", [128, 8, 24])
            cand = sb("cand", [128, 256]); cwk = sb("cwk", [128, 256]); cwk2 = sb("cwk2", [128, 256]); ctop = sb("ctop", [128, 8, 24])
            tau = sb("tau", [128, 8]); thr2 = sb("thr2", [128, 8]); ez = sb("ez", [128, 8, 16]); Z = sb("Z", [128, 8]); base = sb("base", [128, 8])
            thr = sb("thr", [128, 8, 16]); shf = sb("shf", [128, 8, 16]); idf_ = sb("idf_", [128, 8, 16])
            thrT = sb("thrT", [128, PBT]); shT = sb("shT", [128, PBT]); idT = sb("idT", [128, PBT], BF16)
            iob = sb("iob", [128, 128], BF16)
            S.op("dve", "tensor_copy", out=iob[:, :], in_=self.iota[:, :])
            q2rep = [sb(f"q2rep{i}", [128, 16, 128], BF16) for i in range(2)]
            dsh = [sb(f"dsh{i}", [128, 4, 128]) for i in range(2)]
            msk = [sb(f"msk{i}", [128, 4, 128], BF16) for i in range(2)]
            exv = [sb(f"exv{i}", [128, 4, 128], BF16) for i in range(2)]
            rhsG = [sb(f"rhsG{i}", [128, 4, 128], BF16) for i in range(2)]
            lhsG = [sb(f"lhsG{i}", [128, 4, 128], BF16) for i in range(2)]
            Hs = [sb(f"Hs{i}", [128, PBT], BF16) for i in range(2)]
            Wt = [sb(f"Wt{i}", [128, PBT], BF16) for i in range(2)]

            wqsrc = din["w_q"].tensor[l].rearrange("(c p) n -> p c n", p=128)
            for i in range(2):
                S.dma("pool", wq[:, :, i * 1024:(i + 1) * 1024], din["w_q"].v(wqsrc[:, :, i * 1024:(i + 1) * 1024]))
            S.dma("pool", k1T[:, :], din["k1T"].v(din["k1T"].tensor[l]))
            S.dma("pool", k2T[:, :], din["k2T"].v(din["k2T"].tensor[l]))
            S.dma("sp", nfb[:, :], din["norm_ffn"].v(din["norm_ffn"].tensor[l:l + 1, :].partition_broadcast(128)))
            S.dma("sp", nfin[:, :], din["norm_final"].v(din["norm_final"].tensor[0:1, :].partition_broadcast(128)))

            def bc3(buf, ap2, shape):
                return View(buf, ap2.unsqueeze(2).to_broadcast(shape))

            iota3 = View(iob, iob.tensor[:, :].unsqueeze(1).to_broadcast([128, 4, 128]))

            def front(pb):
                t0 = pb * PBT
                sl = pb % 2
                for tt in range(2):
                    xpt = xp[sl * 2 + tt]
                    S.dma("sp", xpt[:, :], self.xs[t0 + tt * 128:t0 + (tt + 1) * 128, :])
                    if split:
                        S.dma("sp", xalt[:, :], self.xs[SEQ // 2 + t0 + tt * 128:SEQ // 2 + t0 + (tt + 1) * 128, :])
                        S.op("dve", "tensor_scalar", out=xpt[:, :], in0=xpt[:, :], scalar1=selb[:, 0:1], scalar2=None, op0=ALU.mult)
                        S.op("dve", "scalar_tensor_tensor", out=xpt[:, :], in0=xalt[:, :], scalar=selb[:, 1:2], in1=xpt[:, :], op0=ALU.mult, op1=ALU.add)
                    self.rmsnorm_tile(xpt[:, :], nfb[:, :], xn[:, :], sq[:, :], ss)
                    pv = self.pbf(tt % 2)
                    for c in range(8):
                        S.op("pe", "transpose", out=pv[:, c * 128:(c + 1) * 128], in_=xn[:, c * 128:(c + 1) * 128], identity=self.idb[:, :])
                    S.op("dve", "tensor_copy", out=xnT[sl][:, :, tt * 128:(tt + 1) * 128], in_=pv.rearrange("p (c t) -> p c t", t=128))
                    yield
                for j in range(16):
                    h, side = j // 2, j % 2
                    bank = j % 2
                    for c in range(8):
                        S.op("pe", "matmul", out=PB[bank][:, 0:PBT], lhsT=wq[:, c, j * 128:(j + 1) * 128], rhs=xnT[sl][:, c, :], start=(c == 0), stop=(c == 7))
                    S.op("dve", "tensor_copy", out=qT[:, side * 8 + h, :], in_=PB[bank][:, 0:PBT])
                    yield
                for tt in range(2):
                    ts_ = slice(tt * 128, (tt + 1) * 128)
                    for half in range(2):
                        for jj in range(8):
                            j = half * 8 + jj
                            S.op("pe", "matmul", out=PB[jj // 4][:, (jj % 4) * 128:(jj % 4 + 1) * 128], lhsT=qT[:, j, ts_],
                                 rhs=(k1T if j < 8 else k2T)[:, :], start=True, stop=True)
                        for b_ in range(2):
                            dst = sc[:, half * 8 + 4 * b_:half * 8 + 4 * b_ + 4, :]
                            S.op("dve", "tensor_copy", out=dst, in_=PB[b_][:, :].rearrange("p (j n) -> p j n", n=128))
                        yield
                    for h in range(8):
                        s1 = sc[:, h, :]; s2 = sc[:, 8 + h, :]
                        S.op("dve", "max", out=v1[:, h, 0:8], in_=s1)
                        S.op("dve", "max", out=v2[:, h, 0:8], in_=s2)
                        S.op("dve", "max_index", out=idx1[:, h, 0:8], in_max=v1[:, h, 0:8], in_values=s1)
                        S.op("dve", "match_replace", out=wk1[:, :], in_to_replace=v1[:, h, 0:8], in_values=s1, imm_value=-1e30)
                        S.op("dve", "match_replace", out=wk2[:, :], in_to_replace=v2[:, h, 0:8], in_values=s2, imm_value=-1e30)
                        S.op("dve", "max", out=v1[:, h, 8:16], in_=wk1[:, :])
                        S.op("dve", "max", out=v2[:, h, 8:16], in_=wk2[:, :])
                        S.op("dve", "max_index", out=idx1[:, h, 8:16], in_max=v1[:, h, 8:16], in_values=wk1[:, :])
                        S.op("dve", "match_replace", out=wk3[:, :], in_to_replace=v2[:, h, 8:16], in_values=wk2[:, :], imm_value=-1e30)
                        S.op("dve", "tensor_tensor", out=cand[:, :].rearrange("p (i j) -> p i j", j=16),
                             in0=View(v1, v1.tensor[:, h, :].unsqueeze(2).to_broadcast([128, 16, 16])),
                             in1=View(v2, v2.tensor[:, h, 0:16].unsqueeze(1).to_broadcast([128, 16, 16])), op=ALU.add)
                        S.op("dve", "max", out=v2[:, h, 16:24], in_=wk3[:, :])
                        S.op("dve", "max", out=ctop[:, h, 0:8], in_=cand[:, :])
                        S.op("dve", "match_replace", out=cwk[:, :], in_to_replace=ctop[:, h, 0:8], in_values=cand[:, :], imm_value=-1e30)
                        S.op("dve", "max", out=ctop[:, h, 8:16], in_=cwk[:, :])
                        S.op("dve", "match_replace", out=cwk2[:, :], in_to_replace=ctop[:, h, 8:16], in_values=cwk[:, :], imm_value=-1e30)
                        S.op("dve", "max", out=ctop[:, h, 16:24], in_=cwk2[:, :])
                        yield
                    S.op("dve", "tensor_tensor", out=tau[:, :], in0=ctop[:, :, 15], in1=ctop[:, :, 16], op=ALU.add)
                    S.op("dve", "tensor_scalar", out=tau[:, :], in0=tau[:, :], scalar1=0.5, scalar2=None, op0=ALU.mult)
                    S.op("dve", "tensor_tensor", out=thr2[:, :], in0=v2[:, :, 15], in1=v2[:, :, 16], op=ALU.add)
                    S.op("dve", "tensor_scalar", out=thr2[:, :], in0=thr2[:, :], scalar1=0.5, scalar2=None, op0=ALU.mult)
                    S.op("dve", "tensor_tensor", out=ez[:, :, :], in0=ctop[:, :, 0:16], in1=bc3(ctop, ctop.tensor[:, :, 0], [128, 8, 16]), op=ALU.subtract)
                    S.op("act", "activation", out=ez[:, :, :], in_=ez[:, :, :], func=AF.Exp)
                    S.op("dve", "reduce_sum", out=Z[:, :], in_=ez[:, :, :], axis=AX.X)
                    S.op("act", "activation", out=Z[:, :], in_=Z[:, :], func=AF.Ln)
                    S.op("dve", "tensor_tensor", out=base[:, :], in0=ctop[:, :, 0], in1=Z[:, :], op=ALU.add)
                    S.op("dve", "tensor_tensor", out=thr[:, :, :], in0=bc3(tau, tau.tensor[:, :], [128, 8, 16]), in1=v1[:, :, :], op=ALU.subtract)
                    S.op("dve", "tensor_tensor", out=thr[:, :, :], in0=thr[:, :, :], in1=bc3(thr2, thr2.tensor[:, :], [128, 8, 16]), op=ALU.max)
                    S.op("dve", "tensor_tensor", out=shf[:, :, :], in0=bc3(base, base.tensor[:, :], [128, 8, 16]), in1=v1[:, :, :], op=ALU.subtract)
                    S.op("dve", "tensor_copy", out=idf_[:, :, :], in_=idx1[:, :, :])
                    for bnk, (srcb, dstb) in enumerate(((thr, thrT), (shf, shT), (idf_, idT))):
                        S.op("pe", "transpose", out=PB[bnk % 2][:, 0:128], in_=srcb[:, :, :].rearrange("p h i -> p (h i)"), identity=self.idf[:, :])
                        S.op("dve", "tensor_copy", out=dstb[:, ts_], in_=PB[bnk % 2][:, 0:128])
                    yield

            def gating(pb):
                NG = PBT // 4

                def stA(n):
                    if n % 4 == 0:
                        tg = n * 4
                        S.op("act", "activation", out=q2rep[(n // 4) % 2][:, :, :].rearrange("p t (h i) -> p t h i", i=16),
                             in_=View(qT, qT.tensor[:, 8:16, tg:tg + 16].rearrange("p h t -> p t h").unsqueeze(3).to_broadcast([128, 16, 8, 16])), func=AF.Copy)
                    qr = q2rep[(n // 4) % 2]
                    for i in range(4):
                        S.op("pe", "matmul", out=PB[n % 2][:, i * 128:(i + 1) * 128], lhsT=qr[:, (n % 4) * 4 + i, :], rhs=k2T[:, :], start=True, stop=True)

                def stB(n):
                    tq, k = n * 4, n % 2
                    p3 = PB[n % 2][:, :].rearrange("p (t e) -> p t e", e=128)
                    S.op("dve", "tensor_tensor", out=dsh[k][:, :, :], in0=p3, in1=bc3(shT, shT.tensor[:, tq:tq + 4], [128, 4, 128]), op=ALU.subtract)
                    S.op("dve", "tensor_tensor", out=msk[k][:, :, :], in0=p3, in1=bc3(thrT, thrT.tensor[:, tq:tq + 4], [128, 4, 128]), op=ALU.is_ge)
                    S.op("act", "activation", out=exv[k][:, :, :], in_=dsh[k][:, :, :], func=AF.Exp)
                    S.op("dve", "tensor_tensor", out=lhsG[k][:, :, :], in0=iota3, in1=bc3(idT, idT.tensor[:, tq:tq + 4], [128, 4, 128]), op=ALU.is_equal)
                    S.op("pool", "tensor_tensor", out=rhsG[k][:, :, :], in0=exv[k][:, :, :], in1=msk[k][:, :, :], op=ALU.mult)

                def stC(n):
                    k = n % 2
                    for i in range(4):
                        S.op("pe", "matmul", out=PB[2 + n % 2][:, i * 128:(i + 1) * 128], lhsT=lhsG[k][:, i, :], rhs=rhsG[k][:, i, :], start=True, stop=True)

                def stD(n):
                    tq = n * 4
                    S.op("act", "activation", out=G[:, :, tq:tq + 4], in_=PB[2 + n % 2][:, :].rearrange("p (t e) -> p e t", e=128), func=AF.Copy)

                for i_ in range(NG + 3):
                    if i_ < NG:
                        stA(i_)
                    if 0 <= i_ - 1 < NG:
                        stB(i_ - 1)
                    if 0 <= i_ - 2 < NG:
                        stC(i_ - 2)
                    if 0 <= i_ - 3 < NG:
                        stD(i_ - 3)

            def expert(pb, gen):
                sl = pb % 2
                NGRP = 128 // GE

                def load_group(gi):
                    e0 = gi * GE
                    S.dma("sp", uTs[gi % 2][:, :, :], self.uTb.v(self.uTb.tensor[e0:e0 + GE].rearrange("g p n -> p g n")))
                    S.dma("sp", vs[gi % 2][:, :, :], self.vrb.v(self.vrb.tensor[e0:e0 + GE].rearrange("g p n -> p g n")))

                def stH(e2):
                    gi, e = e2 // GE, e2 % GE
                    ub = uTs[gi % 2]
                    hb = 2 + e2 % 2
                    for c in range(8):
                        S.op("pe", "matmul", out=PB[hb][:, 0:PBT], lhsT=ub[:, e, c * 128:(c + 1) * 128], rhs=xnT[sl][:, c, :], start=(c == 0), stop=(c == 7))
                    S.op("act", "activation", out=Hs[e2 % 2][:, :], in_=PB[hb][:, 0:PBT], func=AF.Gelu)
                    S.op("pool", "tensor_tensor", out=Wt[e2 % 2][:, :], in0=Hs[e2 % 2][:, :], in1=G[:, e2, :], op=ALU.mult)

                def stO(e2):
                    gi, e = e2 // GE, e2 % GE
                    vb = vs[gi % 2]
                    for tt in range(2):
                        for half in range(2):
                            S.op("pe", "matmul", out=PB[4 + tt * 2 + half][:, :], lhsT=Wt[e2 % 2][:, tt * 128:(tt + 1) * 128],
                                 rhs=vb[:, e, half * 512:(half + 1) * 512], start=(e2 == 0), stop=(e2 == 127))

                load_group(0)
                load_group(1)
                for e2 in range(129):
                    if e2 < 128:
                        stH(e2)
                    if e2 >= 1:
                        stO(e2 - 1)
                        if (e2 - 1) % GE == GE - 1:
                            gnext = (e2 - 1) // GE + 2
                            if gnext < NGRP:
                                load_group(gnext)
                    if gen is not None and e2 % 3 == 0 and e2 >= 3:
                        next(gen, None)
                if gen is not None:
                    for _ in gen:
                        pass

            def epilogue(pb):
                t0 = pb * PBT
                sl = pb % 2
                for tt in range(2):
                    xpt = xp[sl * 2 + tt]
                    for half in range(2):
                        hs = slice(half * 512, (half + 1) * 512)
                        S.op("dve", "tensor_tensor", out=xpt[:, hs], in0=PB[4 + tt * 2 + half][:, :], in1=xpt[:, hs], op=ALU.add)
                    rows = slice(t0 + tt * 128, t0 + (tt + 1) * 128)
                    if last:
                        self.rmsnorm_tile(xpt[:, :], nfin[:, :], yo, sq[:, :], ss)
                        S.dma("sp", self.out[rows, :], yo)
                    else:
                        S.dma("sp", self.xs[rows, :], xpt[:, :])

            for _ in front(0):
                pass
            for pb in range(nblk):
                gating(pb)
                gen = front(pb + 1) if pb + 1 < nblk else None
                expert(pb, gen)
                epilogue(pb)


def _rope_consts():
    inv = np.zeros((128,), np.float32); sgn = np.zeros((128,), np.float32)
    base = (np.float32(500000.0) ** (-np.arange(0, 16, 2, dtype=np.float32) / np.float32(16))).astype(np.float32)
    for r in range(128):
        d = r % 64
        if d < 16:
            inv[r] = base[d % 8]
            sgn[r] = -1.0 if d < 8 else 1.0
    return np.stack([inv, sgn], axis=1).astype(np.float32)


def prep_shared(inp):
    f = lambda a: np.ascontiguousarray(np.asarray(a, dtype=np.float32))
    w_in = f(inp["w_in"])
    perm = np.arange(1024)
    d = perm % 64
    perm = np.where(d < 8, perm + 8, np.where(d < 16, perm - 8, perm))
    w_full = np.concatenate([w_in, w_in[:, :, perm]], axis=2)
    lam = np.concatenate([f(inp["lam_q1"]), f(inp["lam_k1"]), f(inp["lam_q2"]), f(inp["lam_k2"])], axis=1)
    cw = f(inp["conv_w"])
    cw = cw.transpose(0, 2, 1).reshape(L, 2, 128, 3).transpose(0, 2, 1, 3).reshape(L, 128, 6)
    gate = np.concatenate([f(inp["gla_w_gate2"]), f(inp["gla_b_gate"])[:, None, :]], axis=1)
    u = f(inp["peer_u"]).reshape(L, 128, 128, 8, 128)
    uT = np.ascontiguousarray(u.transpose(0, 2, 4, 3, 1)).reshape(L, 128, 128, 1024)
    v = f(inp["peer_v"]).reshape(L, 128, 128, 1024)
    vr = np.ascontiguousarray(v.transpose(0, 2, 1, 3))
    return {
        "norm_mix": f(inp["norm_mix"]), "norm_ffn": f(inp["norm_ffn"]), "norm_final": f(inp["norm_final"]).reshape(1, D),
        "w_in": np.ascontiguousarray(w_full), "lam": np.ascontiguousarray(lam), "diff_norm": f(inp["diff_norm"]),
        "conv_w": np.ascontiguousarray(cw), "gate_aug": np.ascontiguousarray(gate), "gla_norm": f(inp["gla_norm"]),
        "w_out": f(inp["w_out"]), "w_q": f(inp["peer_w_q"]),
        "k1T": np.ascontiguousarray(f(inp["peer_keys1"]).transpose(0, 2, 1)),
        "k2T": np.ascontiguousarray(f(inp["peer_keys2"]).transpose(0, 2, 1)),
        "uT": uT, "vr": vr, "rope_c": _rope_consts(),
    }


def core_inputs(shared, inp, b):
    m = dict(shared)
    m["x"] = np.ascontiguousarray(np.asarray(inp["x"], dtype=np.float32)[b])
    m["pos"] = np.ascontiguousarray(np.asarray(inp["positions"]).astype(np.int32)[b:b + 1])
    m["sel"] = np.array([[1.0, 0.0]], np.float32)
    return m


_PROG = None


def kernel(**inputs):
    global _PROG
    if _PROG is None:
        _PROG = Prog()
    shared = prep_shared(inputs)
    B = np.asarray(inputs["x"]).shape[0]
    in_maps = [core_inputs(shared, inputs, c % B) for c in range(8)]
    for c in range(B, 8):
        in_maps[c]["sel"] = np.array([[0.0, 1.0]], np.float32)
    res = run_bass_kernel_spmd(_PROG.nc, in_maps, core_ids=list(range(8)))
    out = np.stack([np.concatenate([np.asarray(res.results[b]["out"]), np.asarray(res.results[b + B]["out"])], axis=0)
                    for b in range(B)], axis=0)
    return out.astype(np.float32)
```

```python
import numpy as np
import concourse.bass as bass
import concourse.mybir as mybir

F32 = mybir.dt.float32
BF16 = mybir.dt.bfloat16
I32 = mybir.dt.int32
U32 = mybir.dt.uint32
AF = mybir.ActivationFunctionType
ALU = mybir.AluOpType
AX = mybir.AxisListType

EPOCH = 8000
WRITE_KEYS = ("out", "accum_out", "out_max", "out_indices", "ap")


class Buf:
    def __init__(self, S, tensor, name):
        self.S = S
        self.tensor = tensor
        self.name = name
        self.w = {}
        self.r = []

    def __getitem__(self, key):
        return View(self, self.tensor[key])

    def v(self, ap):
        return View(self, ap)


class View:
    def __init__(self, buf, ap):
        self.buf = buf
        self.ap = ap

    def __getitem__(self, key):
        return View(self.buf, self.ap[key])

    def rearrange(self, s, **kw):
        return View(self.buf, self.ap.rearrange(s, **kw))

    def bcast(self, shape):
        return View(self.buf, self.ap.to_broadcast(shape))

    def bitcast(self, dt):
        return View(self.buf, self.ap.bitcast(dt))


class Counter:
    def __init__(self, S, name, unit):
        self.S = S
        self.name = name
        self.unit = unit
        self.n = 0
        self.sems = []
        self.per_epoch = EPOCH // unit

    def sem_for(self, n):
        ep = (n - 1) // self.per_epoch
        while len(self.sems) <= ep:
            self.sems.append(self.S.nc.alloc_semaphore(name=f"{self.name}_e{len(self.sems)}"))
        return self.sems[ep], ((n - 1) % self.per_epoch + 1) * self.unit


ENGS = ("pe", "act", "dve", "pool", "sp")


class Sched:
    def __init__(self, nc, n_dma_slots=8):
        self.nc = nc
        self.cnt = {e: Counter(self, e, 1) for e in ENGS}
        self.prog = {e: [] for e in ENGS}
        self.known = {e: {} for e in ENGS}
        self.dma_slots = {}
        self.dma_rr = {}
        for q in ("sp", "pool", "act"):
            self.dma_slots[q] = [Counter(self, f"dma_{q}{i}", 16) for i in range(n_dma_slots)]
            self.dma_rr[q] = 0
        self.counters = {c.name: c for c in self.cnt.values()}
        for q in self.dma_slots:
            for c in self.dma_slots[q]:
                self.counters[c.name] = c
        self.stack = []

    def _need(self, eng, dep):
        if dep is None:
            return
        cname, n = dep
        if n <= 0:
            return
        if cname == "pe" and eng == "pe":
            return
        k = self.known[eng]
        if k.get(cname, 0) >= n:
            return
        k[cname] = n
        sem, val = self.counters[cname].sem_for(n)
        self.prog[eng].append(("wait", sem, val))

    def _deps(self, eng, reads, writes, waw=True):
        for v in reads:
            for d in v.buf.w.items():
                self._need(eng, d)
        for v in writes:
            if waw:
                for d in v.buf.w.items():
                    self._need(eng, d)
            for r in v.buf.r:
                self._need(eng, r)

    def _commit(self, tag, reads, writes):
        for v in reads:
            v.buf.r.append(tag)
            if len(v.buf.r) > 64:
                d = {}
                for c, n in v.buf.r:
                    d[c] = max(d.get(c, 0), n)
                v.buf.r = list(d.items())
        for v in writes:
            v.buf.w[tag[0]] = tag[1]
            v.buf.r = []

    def op(self, eng, name, *args, **kw):
        reads, writes = [], []
        for k, a in kw.items():
            if isinstance(a, View):
                (writes if k in WRITE_KEYS else reads).append(a)
        for a in args:
            assert not isinstance(a, View), "use kwargs for views"
        extra_r = kw.pop("_reads", [])
        extra_w = kw.pop("_writes", [])
        reads += extra_r
        writes += extra_w
        self._deps(eng, reads, writes)
        c = self.cnt[eng]
        c.n += 1
        n = c.n
        sem, _ = c.sem_for(n)
        rk = {k: (a.ap if isinstance(a, View) else a) for k, a in kw.items()}
        self.prog[eng].append(("op", name, args, rk, sem, 1))
        self._commit((eng, n), reads, writes)
        return (eng, n)

    def dma(self, q, out, in_, waw=True, **kw):
        slots = self.dma_slots[q]
        i = self.dma_rr[q]
        self.dma_rr[q] = (i + 1) % len(slots)
        c = slots[i]
        self._need(q, (c.name, c.n))
        self._deps(q, [in_], [out], waw=waw)
        c.n += 1
        sem, _ = c.sem_for(c.n)
        self.prog[q].append(("dma", out.ap, in_.ap, kw, sem))
        tag = (c.name, c.n)
        self._commit(tag, [in_], [out])
        return tag

    def cc_allgather(self, out, in_, groups):
        q = "pool"
        slots = self.dma_slots[q]
        i = self.dma_rr[q]
        self.dma_rr[q] = (i + 1) % len(slots)
        c = slots[i]
        self._need(q, (c.name, c.n))
        self._deps(q, [in_], [out])
        c.n += 1
        sem, _ = c.sem_for(c.n)
        self.prog[q].append(("cc", out.ap, in_.ap, groups, sem))
        tag = (c.name, c.n)
        self._commit(tag, [in_], [out])
        return tag

    def barrier(self):
        for e in ENGS:
            for cname, c in self.counters.items():
                if cname == e and e == "pe":
                    continue
                self._need(e, (cname, c.n))

    def wait_all(self, eng):
        for cname, c in self.counters.items():
            self._need(eng, (cname, c.n))

    def emit(self, block):
        nc = self.nc
        S = self

        def run(engname, eng):
            for item in S.prog[engname]:
                if item[0] == "wait":
                    eng.wait_ge(item[1], item[2])
                elif item[0] == "op":
                    _, name, args, kw, sem, inc = item
                    ins = getattr(eng, name)(*args, **kw)
                    ins.then_inc(sem, inc)
                elif item[0] == "dma":
                    _, o, i, kw, sem = item
                    eng.dma_start(out=o, in_=i, **kw).then_inc(sem, 16)
                elif item[0] == "cc":
                    _, o, i, groups, sem = item
                    eng.collective_compute("AllGather", op=ALU.bypass, replica_groups=groups, ins=[i], outs=[o]).then_inc(sem, 16)

        @block.tensor
        def _(e):
            run("pe", e)

        @block.scalar
        def _(e):
            run("act", e)

        @block.vector
        def _(e):
            run("dve", e)

        @block.gpsimd
        def _(e):
            run("pool", e)

        @block.sync
        def _(e):
            run("sp", e)


from contextlib import ExitStack
from concourse.bass_utils import run_bass_kernel_spmd

D = 1024
SEQ = 4096
L = 2
EPS = 1e-6
NBLK = 8
PBT = 256
C_Q, C_K, C_V, C_SB, C_SC, C_SH, C_GQ, C_GK, C_GV, C_GR, C_LR, C_QS, C_KS = (
    0, 512, 1024, 1536, 1792, 2048, 2304, 2432, 2560, 2816, 3072, 3088, 3600)
INW = 4112
PI = float(np.pi)


class Prog:
    def __init__(self, n_layers=L, do_peer=True, dbg=(), stage=9, nblk=NBLK, peer_test=0):
        self.stage = stage
        self.nblk = nblk
        self.n_layers = n_layers
        self.do_peer = do_peer
        self.dbg = dbg
        nc = self.nc = bass.Bass("TRN2", target_bir_lowering=False)
        S = self.S = Sched(nc)
        self.din = {}
        self.dout = {}

        def din(name, shape, dt=F32):
            self.din[name] = Buf(S, nc.dram_tensor(name, list(shape), dt, kind="ExternalInput").ap(), name)
            return self.din[name]

        din("x", [SEQ, D]); din("pos", [1, SEQ], I32)
        din("norm_mix", [L, D]); din("norm_ffn", [L, D]); din("norm_final", [1, D])
        din("w_in", [L, D, INW]); din("lam", [L, 256]); din("diff_norm", [L, 128])
        din("conv_w", [L, 128, 6]); din("gate_aug", [L, 17, 128]); din("gla_norm", [L, 64])
        din("w_out", [L, D, D]); din("w_q", [L, D, 2048]); din("k1T", [L, 128, 128]); din("k2T", [L, 128, 128])
        din("uT", [L, 128, 128, 1024]); din("vr", [L, 128, 128, 1024]); din("rope_c", [128, 2])
        self.split_last = bool(do_peer and not peer_test and n_layers == L)
        din("sel", [1, 2])
        self.out = Buf(S, nc.dram_tensor("out", [SEQ // 2 if self.split_last else SEQ, D], F32, kind="ExternalOutput").ap(), "out")
        self.xs = Buf(S, nc.dram_tensor("xs", [SEQ, D], F32, kind="Internal").ap(), "xs")
        self.uTb = Buf(S, nc.dram_tensor("uTb", [128, 128, 1024], BF16, kind="Internal").ap(), "uTb")
        self.vrb = Buf(S, nc.dram_tensor("vrb", [128, 128, 1024], BF16, kind="Internal").ap(), "vrb")
        for name, shape in dbg:
            self.dout[name] = Buf(S, nc.dram_tensor(name, list(shape), F32, kind="ExternalOutput").ap(), name)

        with ExitStack() as top:
            self.top = top
            self.PB = [Buf(S, top.enter_context(nc.psum_tensor(f"pb{i}", [128, 512], F32)), f"pb{i}") for i in range(8)]
            self.consts()
            if peer_test:
                with ExitStack() as es2:
                    cp = [self.sb(es2, f"cq{i}", [128, D]) for i in range(2)]
                    for i in range(SEQ // 128):
                        S.dma("sp", cp[i % 2][:, :], self.din["x"][i * 128:(i + 1) * 128, :])
                        S.dma("sp", self.xs[i * 128:(i + 1) * 128, :], cp[i % 2][:, :])
                    S.barrier()
                self.convert_weights(0, 0, 1)
                self.peer(0, last=False, nblk=peer_test)
                S.barrier()
                n_layers = 0
                do_peer = False
            for l in range(n_layers):
                src = self.din["x"] if l == 0 else self.xs
                self.mixer(l, src)
                S.barrier()
                if do_peer:
                    self.peer(l, last=(l == n_layers - 1))
                    S.barrier()
            if not do_peer:
                with ExitStack() as es2:
                    cp = [self.sb(es2, f"cp{i}", [128, D]) for i in range(2)]
                    for i in range(SEQ // 128):
                        S.dma("sp", cp[i % 2][:, :], self.xs[i * 128:(i + 1) * 128, :])
                        S.dma("sp", self.out[i * 128:(i + 1) * 128, :], cp[i % 2][:, :])
                    S.wait_all("sp")
            S.wait_all("sp")
            S.wait_all("pool")
            with nc.Block() as block:
                S.emit(block)

    def sb(self, es, name, shape, dt=F32):
        return Buf(self.S, es.enter_context(self.nc.sbuf_tensor(name, list(shape), dt)), name)

    def pbf(self, i, n=1024):
        b = self.PB[i]
        return View(b, b.tensor[:, :].bitcast(BF16))[:, 0:n]

    def tap(self, name, view):
        if name in self.dout:
            self.S.dma("sp", self.dout[name].v(self.dout[name].tensor), view)

    def consts(self):
        S, es = self.S, self.top
        self.idf = self.sb(es, "idf", [128, 128]); self.idb = self.sb(es, "idb", [128, 128], BF16)
        S.op("pool", "memset", ap=self.idf[:, :], constant=0.0)
        S.op("pool", "affine_select", out=self.idf[:, :], in_=self.idf[:, :], pattern=[[-1, 128]],
             compare_op=ALU.not_equal, fill=1.0, base=0, channel_multiplier=1)
        S.op("dve", "tensor_copy", out=self.idb[:, :], in_=self.idf[:, :])
        self.iota = self.sb(es, "iota", [128, 128])
        S.op("pool", "iota", out=self.iota[:, :], pattern=[[1, 128]], base=0, channel_multiplier=0,
             allow_small_or_imprecise_dtypes=True)
        self.ropec = self.sb(es, "ropec", [128, 2])
        S.dma("sp", self.ropec[:, :], self.din["rope_c"][:, :])

    def convert_weights(self, l, part, nparts):
        S, din = self.S, self.din
        n = 64 // nparts
        for i in range(part * n, (part + 1) * n):
            for srcn, dst in (("uT", self.uTb), ("vr", self.vrb)):
                dv = dst.tensor[:, :, :].rearrange("a b (c d) -> (a b c) d", d=4096) if False else dst.tensor[:, :, :].rearrange("a (b2 b4) c -> (a b2) (b4 c)", b4=4)
                sv = din[srcn].tensor[l].rearrange("a (b2 b4) c -> (a b2) (b4 c)", b4=4)
                S.dma("pool", dst.v(dv[64 * i:64 * i + 64, :]), din[srcn].v(sv[64 * i:64 * i + 64, :]), waw=(i == 0))

    def rmsnorm_tile(self, x, gbc, hout, sq, ss, width=D):
        S = self.S
        S.op("act", "activation", out=sq, in_=x, func=AF.Square, accum_out=ss[:, 0:1])
        S.op("act", "activation", out=ss[:, 1:2], in_=ss[:, 0:1], func=AF.Sqrt, scale=1.0 / width, bias=self.epsb[:, 0:1])
        S.op("dve", "reciprocal", out=ss[:, 1:2], in_=ss[:, 1:2])
        S.op("dve", "scalar_tensor_tensor", out=hout, in0=x, scalar=ss[:, 1:2], in1=gbc, op0=ALU.mult, op1=ALU.mult)

    def mixer(self, l, src):
        S, nc, PB = self.S, self.nc, self.PB
        din = self.din
        lam_init = 0.8 - 0.6 * float(np.exp(-0.3 * l))
        with ExitStack() as es:
            sb = lambda name, shape, dt=F32: self.sb(es, f"m{l}_{name}", shape, dt)
            self.epsb = sb("epsb", [128, 1]); S.op("pool", "memset", ap=self.epsb[:, :], constant=EPS)
            wbuf = [sb(f"wbuf{i}", [128, 8, 1024], BF16) for i in range(2)]
            wout = sb("wout", [128, 8, D], BF16)
            KT = sb("KT", [128, 4, SEQ], BF16)
            VA = sb("VA", [128, 32, 4, 130], BF16)
            gbc = sb("gbc", [128, D]); gnb = sb("gnb", [128, 64])
            lams = sb("lams", [128, 8])
            cw = sb("cw", [128, 6]); gate = sb("gate", [17, 128])
            xt = [sb(f"xt{i}", [128, D]) for i in range(2)]
            htm = sb("htm", [128, D], BF16); sq = htm; ss = sb("ss", [128, 2])
            hT = sb("hT", [128, 8, 512], BF16)
            QT = sb("QT", [128, 4, 2, 512], BF16)
            S.op("pool", "memset", ap=QT[:, :, :, :], constant=0.0)
            mixT = sb("mixT", [128, 8, 512], BF16)
            posi = sb("posi", [128, 512], I32); ang = sb("ang", [128, 512]); ang2 = sb("ang2", [128, 512])
            angi = posi; wk = sb("wk", [128, 512])
            Ct = sb("Ct", [128, 512]); St = sb("St", [128, 512])
            t1 = sb("t1", [128, 512]); t2 = sb("t2", [128, 512]); lamt = t2
            zb = sb("zb", [128, 2, 514]); csb = sb("csb", [128, 512]); yb = sb("yb", [128, 512])
            gqT = sb("gqT", [128, 512]); gkT = sb("gkT", [128, 512]); glr = sb("glr", [17, 512])
            tri01 = sb("tri01", [64, 4, 64], BF16); triS = sb("triS", [64, 64]); bd = sb("bd", [128, 4, 64])
            ex = sb("ex", [64, 128]); sp = sb("sp", [64, 128])
            eb = [sb(f"eb{i}", [128, 64]) for i in range(2)]; enb = sb("enb", [128, 64])
            qtl = [sb(f"qtl{i}", [128, 64], BF16) for i in range(2)]; ktl = sb("ktl", [128, 64], BF16)
            qbd = sb("qbd", [128, 4, 64], BF16); attnT = [sb(f"attnT{i}", [64, 4, 64], BF16) for i in range(2)]
            ktm = [sb(f"ktm{i}", [64, 128], BF16) for i in range(2)]; gv = [sb(f"gv{i}", [64, 256], BF16) for i in range(2)]
            sil = [sb(f"sil{i}", [64, 256]) for i in range(2)]
            S32 = sb("S32", [128, 256]); Sbd = sb("Sbd", [128, 256], BF16); tkv = t1[:, 0:256]
            osq = sb("osq", [64, 256]); oss = sb("oss", [64, 8]); og = osq; ogb = sb("ogb", [64, 256], BF16)
            PT = [sb(f"PT{i}", [128, 512], BF16) for i in range(4)]
            accs = [sb(f"acc{i}", [128, 512]) for i in range(2)]
            onesf = sb("onesf", [128, 128]); dnT = sb("dnT", [128, 1])
            S.op("pool", "memset", ap=onesf[:, :], constant=1.0)

            S.dma("pool", wout[:, :, :], din["w_out"].v(din["w_out"].tensor[l].rearrange("(c p) n -> p c n", p=128)))
            S.dma("sp", gbc[:, :], din["norm_mix"].v(din["norm_mix"].tensor[l:l + 1, :].partition_broadcast(128)))
            S.dma("sp", gnb[:, :], din["gla_norm"].v(din["gla_norm"].tensor[l:l + 1, :].partition_broadcast(128)))
            S.dma("sp", lamt[:, 0:256], din["lam"].v(din["lam"].tensor[l:l + 1, :].partition_broadcast(128)))
            S.dma("sp", cw[:, :], din["conv_w"].v(din["conv_w"].tensor[l]))
            S.dma("sp", gate[:, :], din["gate_aug"].v(din["gate_aug"].tensor[l]))
            S.dma("sp", dnT[:, :], din["diff_norm"].v(din["diff_norm"].tensor[l, :].rearrange("(p o) -> p o", o=1)))
            S.op("dve", "tensor_scalar", out=dnT[:, :], in0=dnT[:, :], scalar1=1.0 - lam_init, scalar2=None, op0=ALU.mult)
            S.op("dve", "tensor_tensor", out=lamt[:, 0:64], in0=lamt[:, 0:64], in1=lamt[:, 64:128], op=ALU.mult)
            S.op("dve", "tensor_tensor", out=lamt[:, 128:192], in0=lamt[:, 128:192], in1=lamt[:, 192:256], op=ALU.mult)
            S.op("dve", "reduce_sum", out=lams[:, 0:1], in_=lamt[:, 0:64], axis=AX.X)
            S.op("dve", "reduce_sum", out=lams[:, 1:2], in_=lamt[:, 128:192], axis=AX.X)
            S.op("act", "activation", out=lams[:, 2:4], in_=lams[:, 0:2], func=AF.Exp)
            S.op("dve", "tensor_tensor", out=lams[:, 4:5], in0=lams[:, 3:4], in1=lams[:, 2:3], op=ALU.subtract)
            S.op("dve", "tensor_scalar", out=lams[:, 5:6], in0=lams[:, 4:5], scalar1=-lam_init, scalar2=None, op0=ALU.add)
            nlam = lams[:, 5:6]
            S.op("pool", "memset", ap=tri01[:, :, :], constant=1.0)
            S.op("pool", "affine_select", out=tri01[:, :, :], in_=tri01[:, :, :], pattern=[[0, 4], [1, 64]],
                 compare_op=ALU.is_ge, fill=0.0, base=0, channel_multiplier=-1)
            S.op("pool", "memset", ap=triS[:, :], constant=-1.0 / 16.0)
            S.op("pool", "affine_select", out=triS[:, :], in_=triS[:, :], pattern=[[1, 64]],
                 compare_op=ALU.is_ge, fill=0.0, base=0, channel_multiplier=-1)
            S.op("pool", "memset", ap=bd[:, :, :], constant=0.0)
            for h in range(4):
                S.op("pool", "memset", ap=bd[32 * h:32 * h + 32, h, :], constant=1.0)
            S.op("pool", "memset", ap=VA[:, :, :, 128:130], constant=1.0)
            S.op("pool", "memset", ap=glr[:, :], constant=1.0)
            S.op("pool", "memset", ap=zb[:, :, :], constant=0.0)
            S.op("pool", "memset", ap=S32[:, :], constant=0.0)
            S.op("pool", "memset", ap=Sbd[:, :], constant=0.0)

            W_IN = din["w_in"]
            wsrc = W_IN.tensor[l].rearrange("(c p) n -> p c n", p=128)
            wstate = {"n": 0}

            def load_w(cols):
                b = wbuf[wstate["n"] % 2]
                wstate["n"] += 1
                offs = []
                o = 0
                for ci_, (c0, n) in enumerate(cols):
                    S.dma("pool", b[:, :, o:o + n], W_IN.v(wsrc[:, :, c0:c0 + n]), waw=(ci_ == 0))
                    offs.append(o)
                    o += n
                return b, offs

            def proj_fm(bank, wb, off, m, n=512):
                for c in range(8):
                    S.op("pe", "matmul", out=PB[bank][0:m, 0:n], lhsT=wb[:, c, off:off + m], rhs=hT[:, c, 0:n],
                         start=(c == 0), stop=(c == 7))

            def reduce_angle(a):
                S.op("dve", "tensor_scalar", out=angi[:, :], in0=a[:, :], scalar1=1.0 / (2 * PI), scalar2=None, op0=ALU.mult)
                S.op("dve", "tensor_copy", out=wk[:, :], in_=angi[:, :])
                S.op("dve", "scalar_tensor_tensor", out=a[:, :], in0=wk[:, :], scalar=-2 * PI, in1=a[:, :], op0=ALU.mult, op1=ALU.add)
                S.op("dve", "tensor_scalar", out=wk[:, :], in0=a[:, :], scalar1=PI, scalar2=-2 * PI, op0=ALU.is_gt, op1=ALU.mult)
                S.op("dve", "tensor_tensor", out=a[:, :], in0=a[:, :], in1=wk[:, :], op=ALU.add)
                S.op("dve", "tensor_scalar", out=wk[:, :], in0=a[:, :], scalar1=-PI, scalar2=2 * PI, op0=ALU.is_lt, op1=ALU.mult)
                S.op("dve", "tensor_tensor", out=a[:, :], in0=a[:, :], in1=wk[:, :], op=ALU.add)

            XS = src
            for jb in range(self.nblk):
                t0 = jb * 512
                for tt in range(4):
                    xb_ = xt[tt % 2]
                    S.dma("sp", xb_[:, :], XS[t0 + tt * 128:t0 + (tt + 1) * 128, :])
                    self.rmsnorm_tile(xb_[:, :], gbc[:, :], htm[:, :], sq[:, :], ss)
                    bank = tt % 2
                    pv = self.pbf(bank)
                    for c in range(8):
                        S.op("pe", "transpose", out=pv[:, c * 128:(c + 1) * 128], in_=htm[:, c * 128:(c + 1) * 128], identity=self.idb[:, :])
                    S.op("act", "activation", out=hT[:, :, tt * 128:(tt + 1) * 128],
                         in_=pv.rearrange("p (c t) -> p c t", t=128), func=AF.Copy)
                if self.stage < 2:
                    continue
                S.dma("sp", posi[:, :], din["pos"].v(din["pos"].tensor[0:1, t0:t0 + 512].partition_broadcast(128)))
                S.op("dve", "tensor_copy", out=ang[:, :], in_=posi[:, :])
                S.op("dve", "tensor_scalar", out=ang[:, :], in0=ang[:, :], scalar1=self.ropec[:, 0:1], scalar2=None, op0=ALU.mult)
                S.op("dve", "tensor_scalar", out=ang2[:, :], in0=ang[:, :], scalar1=PI / 2, scalar2=None, op0=ALU.add)
                reduce_angle(ang)
                reduce_angle(ang2)
                S.op("act", "activation", out=St[:, :], in_=ang[:, :], func=AF.Sin)
                S.op("act", "activation", out=Ct[:, :], in_=ang2[:, :], func=AF.Sin)
                S.op("dve", "tensor_scalar", out=St[:, :], in0=St[:, :], scalar1=self.ropec[:, 1:2], scalar2=None, op0=ALU.mult)
                for which, (c_a, c_s) in enumerate(((C_Q, C_QS), (C_K, C_KS))):
                    wb, offs = load_w([(c_a, 512), (c_s, 512)])
                    for i in range(4):
                        ba, bb = (2 * i) % 8, (2 * i + 1) % 8
                        proj_fm(ba, wb, offs[0] + i * 128, 128)
                        proj_fm(bb, wb, offs[1] + i * 128, 128)
                        S.op("dve", "tensor_tensor", out=t1[:, :], in0=PB[ba][:, :], in1=Ct[:, :], op=ALU.mult)
                        S.op("dve", "tensor_tensor", out=t2[:, :], in0=PB[bb][:, :], in1=St[:, :], op=ALU.mult)
                        if which == 0:
                            S.op("pool", "tensor_tensor", out=QT[0:64, i, 0, :], in0=t1[0:64, :], in1=t2[0:64, :], op=ALU.add)
                            S.op("pool", "tensor_tensor", out=QT[64:128, i, 1, :], in0=t1[64:128, :], in1=t2[64:128, :], op=ALU.add)
                        else:
                            S.op("pool", "tensor_tensor", out=KT[:, i, t0:t0 + 512], in0=t1[:, :], in1=t2[:, :], op=ALU.add)
                if self.stage < 3:
                    continue
                wb, offs = load_w([(C_SB, 768)])
                for ch in range(2):
                    proj_fm(0, wb, 256 + ch * 128, 128)
                    proj_fm(1, wb, 512 + ch * 128, 128)
                    proj_fm(2, wb, 0 + ch * 128, 128)
                    S.op("act", "activation", out=csb[:, :], in_=PB[0][:, :], func=AF.Copy)
                    S.op("dve", "tensor_tensor", out=zb[:, ch, 2:514], in0=csb[:, :], in1=PB[1][:, :], op=ALU.mult)
                    S.op("dve", "tensor_scalar", out=yb[:, :], in0=zb[:, ch, 0:512], scalar1=cw[:, ch * 3 + 0:ch * 3 + 1], scalar2=None, op0=ALU.mult)
                    S.op("dve", "scalar_tensor_tensor", out=yb[:, :], in0=zb[:, ch, 1:513], scalar=cw[:, ch * 3 + 1:ch * 3 + 2], in1=yb[:, :], op0=ALU.mult, op1=ALU.add)
                    S.op("dve", "scalar_tensor_tensor", out=yb[:, :], in0=zb[:, ch, 2:514], scalar=cw[:, ch * 3 + 2:ch * 3 + 3], in1=yb[:, :], op0=ALU.mult, op1=ALU.add)
                    S.op("dve", "tensor_tensor", out=mixT[:, 4 + ch, :], in0=PB[2][:, :], in1=yb[:, :], op=ALU.mult)
                    S.op("dve", "tensor_copy", out=zb[:, ch, 0:2], in_=zb[:, ch, 512:514])
                wb, offs = load_w([(C_GQ, 256), (C_LR, 16), (C_V, 512)])
                proj_fm(3, wb, 0, 128); S.op("act", "activation", out=gqT[:, :], in_=PB[3][:, :], func=AF.Copy)
                proj_fm(4, wb, 128, 128); S.op("act", "activation", out=gkT[:, :], in_=PB[4][:, :], func=AF.Copy)
                proj_fm(5, wb, 256, 16); S.op("act", "activation", out=glr[0:16, :], in_=PB[5][0:16, :], func=AF.Copy)
                for tt in range(4):
                    bank = 6 + tt % 2
                    for c in range(8):
                        S.op("pe", "matmul", out=PB[bank][:, :], lhsT=hT[:, c, tt * 128:(tt + 1) * 128], rhs=wb[:, c, 272:784],
                             start=(c == 0), stop=(c == 7))
                    S.op("act", "activation", out=VA[:, jb * 4 + tt, :, 0:128],
                         in_=PB[bank][:, :].rearrange("p (h e) -> p h e", e=128), func=AF.Copy)
                wb, offs = load_w([(C_GV, 512)])
                if self.stage < 4:
                    continue
                def gla_s1(ck):
                    tc = ck * 64
                    k = ck % 2
                    S.op("pe", "matmul", out=PB[0][0:64, 0:128], lhsT=glr[0:17, tc:tc + 64], rhs=gate[0:17, :], start=True, stop=True)
                    yield
                    S.op("act", "activation", out=ex[:, :], in_=PB[0][0:64, 0:128], func=AF.Exp, scale=-1.0)
                    yield
                    S.op("act", "activation", out=sp[:, :], in_=ex[:, :], func=AF.Ln, bias=1.0)
                    yield
                    S.op("pe", "matmul", out=PB[1][:, 0:64], lhsT=sp[:, :], rhs=triS[:, :], start=True, stop=True)
                    yield
                    S.op("act", "activation", out=eb[k][:, :], in_=PB[1][:, 0:64], func=AF.Exp)
                    S.op("act", "activation", out=enb[:, :], in_=PB[1][:, 0:64], func=AF.Exp, scale=-1.0)
                    yield
                    S.op("dve", "scalar_tensor_tensor", out=qtl[k][:, :], in0=gqT[:, tc:tc + 64], scalar=32 ** -0.5, in1=eb[k][:, :], op0=ALU.mult, op1=ALU.mult)
                    S.op("dve", "tensor_tensor", out=ktl[:, :], in0=gkT[:, tc:tc + 64], in1=enb[:, :], op=ALU.mult)
                    yield
                    S.op("pool", "tensor_tensor", out=qbd[:, :, :], in0=View(qtl[k], qtl[k].tensor[:, :].unsqueeze(1).to_broadcast([128, 4, 64])),
                         in1=bd[:, :, :], op=ALU.mult)
                    pv3 = self.pbf(3)
                    S.op("pe", "transpose", out=pv3[0:64, 0:128], in_=ktl[:, :], identity=self.idb[:, :])
                    yield
                    S.op("act", "activation", out=ktm[k][:, :], in_=pv3[0:64, 0:128], func=AF.Copy)
                    S.op("pe", "matmul", out=PB[2][0:64, 0:256], lhsT=ktl[:, :], rhs=qbd[:, :, :].rearrange("p h i -> p (h i)"), start=True, stop=True)
                    yield
                    S.op("dve", "tensor_tensor", out=attnT[k][:, :, :], in0=PB[2][0:64, 0:256].rearrange("p (h i) -> p h i", i=64), in1=tri01[:, :, :], op=ALU.mult)
                    for c in range(8):
                        S.op("pe", "matmul", out=PB[4][0:64, :], lhsT=hT[:, c, tc:tc + 64], rhs=wb[:, c, 0:512], start=(c == 0), stop=(c == 7))
                    yield
                    S.op("act", "activation", out=gv[k][:, :], in_=PB[4][0:64, 0:256], func=AF.Copy)
                    yield
                    S.op("act", "activation", out=sil[k][:, :], in_=PB[4][0:64, 256:512], func=AF.Silu)
                    yield

                def gla_s2(ck):
                    tc = ck * 64
                    k = ck % 2
                    S.op("pe", "matmul", out=PB[5][0:64, 0:256], lhsT=qtl[k][:, :], rhs=Sbd[:, :], start=True, stop=True)
                    for h in range(4):
                        S.op("pe", "matmul", out=PB[5][0:64, h * 64:(h + 1) * 64], lhsT=attnT[k][:, h, :], rhs=gv[k][:, h * 64:(h + 1) * 64],
                             start=False, stop=(h == 3), skip_group_check=True)
                    S.op("pe", "matmul", out=PB[6][:, 0:256], lhsT=ktm[k][:, :], rhs=gv[k][:, :], start=True, stop=True)
                    yield
                    S.op("dve", "scalar_tensor_tensor", out=tkv, in0=PB[6][:, 0:256], scalar=eb[k][:, 63:64],
                         in1=bd[:, :, :].rearrange("p h e -> p (h e)"), op0=ALU.mult, op1=ALU.mult)
                    S.op("act", "activation", out=osq[:, :], in_=PB[5][0:64, 0:256], func=AF.Square)
                    yield
                    S.op("dve", "scalar_tensor_tensor", out=S32[:, :], in0=S32[:, :], scalar=eb[k][:, 63:64], in1=tkv, op0=ALU.mult, op1=ALU.add)
                    yield
                    S.op("pool", "tensor_copy", out=Sbd[:, :], in_=S32[:, :])
                    S.op("dve", "reduce_sum", out=oss[:, 0:4], in_=osq[:, :].rearrange("p (h e) -> p h e", e=64), axis=AX.X)
                    yield
                    S.op("act", "activation", out=oss[:, 4:8], in_=oss[:, 0:4], func=AF.Sqrt, scale=1.0 / 64, bias=self.epsb[0:64, 0:1])
                    yield
                    S.op("dve", "reciprocal", out=oss[:, 4:8], in_=oss[:, 4:8])
                    yield
                    S.op("dve", "tensor_tensor", out=og[:, :].rearrange("p (h e) -> p h e", e=64),
                         in0=PB[5][0:64, 0:256].rearrange("p (h e) -> p h e", e=64),
                         in1=View(oss, oss.tensor[0:64, 4:8].unsqueeze(2).to_broadcast([64, 4, 64])), op=ALU.mult)
                    yield
                    S.op("pool", "tensor_tensor", out=og[:, :].rearrange("p (h e) -> p h e", e=64),
                         in0=og[:, :].rearrange("p (h e) -> p h e", e=64),
                         in1=View(gnb, gnb.tensor[0:64, :].unsqueeze(1).to_broadcast([64, 4, 64])), op=ALU.mult)
                    yield
                    S.op("dve", "tensor_tensor", out=ogb[:, :], in0=og[:, :], in1=sil[k][:, :], op=ALU.mult)
                    yield
                    pv7 = self.pbf(7)
                    for hh in range(2):
                        S.op("pe", "transpose", out=pv7[:, hh * 64:(hh + 1) * 64], in_=ogb[:, hh * 128:(hh + 1) * 128], identity=self.idb[0:64, 0:64])
                    yield
                    S.op("act", "activation", out=mixT[:, 6:8, tc:tc + 64], in_=pv7[:, 0:128].rearrange("p (h t) -> p h t", t=64), func=AF.Copy)
                    yield

                for ck in range(-1, 8):
                    g1 = gla_s1(ck + 1) if ck + 1 < 8 else iter(())
                    g2 = gla_s2(ck) if ck >= 0 else iter(())
                    d1 = d2 = False
                    while not (d1 and d2):
                        if not d1:
                            d1 = next(g1, "done") == "done"
                        if not d2:
                            d2 = next(g2, "done") == "done"
                if self.stage < 5:
                    continue
                if self.do_peer:
                    self.convert_weights(l, jb, NBLK)
                nkt = 4 * jb + 4
                items = [(h, m, kt) for h in range(4) for m in range(2) for kt in range(nkt)]

                def obank(h, m):
                    return 3 + 2 * (h % 2) + m

                def st1(i):
                    h, m, kt = items[i]
                    pr = slice(64 * m, 64 * m + 64)
                    o = kt - 4 * jb
                    q0 = max(o, 0) * 128
                    S.op("pe", "matmul", out=PB[i % 3][:, q0:512], lhsT=KT[:, h, kt * 128:(kt + 1) * 128], rhs=QT[:, h, m, q0:512],
                         start=True, stop=True)
                    p = PT[i % 4]
                    S.op("act", "activation", out=p[:, q0:512], in_=PB[i % 3][:, q0:512], func=AF.Exp, scale=0.125)
                    if o >= 0:
                        S.op("pool", "affine_select", out=p[:, q0:q0 + 128], in_=p[:, q0:q0 + 128], pattern=[[1, 128]],
                             compare_op=ALU.is_ge, fill=0.0, base=0, channel_multiplier=-1)

                def st2(i):
                    h, m, kt = items[i]
                    o = kt - 4 * jb
                    q0 = max(o, 0) * 128
                    p = PT[i % 4]
                    acc = accs[m]
                    S.op("pe", "matmul", out=PB[obank(h, m)][:, q0:512], lhsT=VA[:, kt, h, 0:128], rhs=p[:, q0:512],
                         start=(kt == 0), stop=(kt == nkt - 1))
                    accB = (gqT, gkT)[m]
                    if kt % 2 == 0:
                        if kt == 0:
                            S.op("dve", "tensor_copy", out=acc[:, :], in_=p[:, :])
                        else:
                            S.op("dve", "tensor_tensor", out=acc[:, q0:512], in0=acc[:, q0:512], in1=p[:, q0:512], op=ALU.add)
                    else:
                        if kt == 1:
                            if q0 > 0:
                                S.op("dve", "memset", ap=accB[:, 0:q0], constant=0.0)
                            S.op("dve", "tensor_copy", out=accB[:, q0:512], in_=p[:, q0:512])
                        else:
                            S.op("dve", "tensor_tensor", out=accB[:, q0:512], in0=accB[:, q0:512], in1=p[:, q0:512], op=ALU.add)
                    if kt == nkt - 1:
                        rinv = t1 if m == 0 else t2
                        S.op("pe", "matmul", out=PB[7][:, :], lhsT=onesf[:, :], rhs=acc[:, :], start=True, stop=False)
                        S.op("pe", "matmul", out=PB[7][:, :], lhsT=onesf[:, :], rhs=accB[:, :], start=False, stop=True)
                        S.op("act", "activation", out=rinv[:, :], in_=PB[7][:, :], func=AF.Ln)
                        S.op("act", "activation", out=rinv[:, :], in_=rinv[:, :], func=AF.Exp, scale=-1.0)
                        if m == 1:
                            epilogue(h)

                def epilogue(h):
                    S.op("dve", "tensor_tensor", out=csb[:, :], in0=PB[obank(h, 0)][:, :], in1=t1[:, :], op=ALU.mult)
                    S.op("dve", "tensor_tensor", out=yb[:, :], in0=PB[obank(h, 1)][:, :], in1=t2[:, :], op=ALU.mult)
                    S.op("dve", "scalar_tensor_tensor", out=yb[:, :], in0=yb[:, :], scalar=nlam, in1=csb[:, :], op0=ALU.mult, op1=ALU.add)
                    S.op("act", "activation", out=wk[:, :], in_=yb[:, :], func=AF.Square)
                    S.op("pe", "matmul", out=PB[7][:, :], lhsT=onesf[:, :], rhs=wk[:, :], start=True, stop=True)
                    S.op("act", "activation", out=ang[:, :], in_=PB[7][:, :], func=AF.Ln, scale=1.0 / 128, bias=self.epsb[:, 0:1])
                    S.op("act", "activation", out=ang[:, :], in_=ang[:, :], func=AF.Exp, scale=-0.5)
                    S.op("dve", "scalar_tensor_tensor", out=mixT[:, h, :], in0=yb[:, :], scalar=dnT[:, 0:1], in1=ang[:, :], op0=ALU.mult, op1=ALU.mult)

                for i in range(len(items) + 2):
                    if i < len(items):
                        st1(i)
                    if i >= 2:
                        st2(i - 2)
                if "mixT" in self.dout and jb == 0:
                    pass
                for tt in range(4):
                    xb_ = xt[tt % 2]
                    S.dma("sp", xb_[:, :], XS[t0 + tt * 128:t0 + (tt + 1) * 128, :])
                    for half in range(2):
                        bank = 2 + (tt % 2) * 2 + half
                        for c in range(8):
                            S.op("pe", "matmul", out=PB[bank][:, :], lhsT=mixT[:, c, tt * 128:(tt + 1) * 128], rhs=wout[:, c, half * 512:(half + 1) * 512],
                                 start=(c == 0), stop=(c == 7))
                        S.op("dve", "tensor_tensor", out=xb_[:, half * 512:(half + 1) * 512], in0=PB[bank][:, :], in1=xb_[:, half * 512:(half + 1) * 512], op=ALU.add)
                    S.dma("sp", self.xs[t0 + tt * 128:t0 + (tt + 1) * 128, :], xb_[:, :])

    def peer(self, l, last, nblk=None):
        S, nc, PB = self.S, self.nc, self.PB
        din = self.din
        GE = 2
        split = bool(last and self.split_last)
        nblk = ((SEQ // PBT) // (2 if split else 1)) if nblk is None else nblk
        with ExitStack() as es:
            sb = lambda name, shape, dt=F32: self.sb(es, f"p{l}_{name}", shape, dt)
            self.epsb = sb("epsb", [128, 1]); S.op("pool", "memset", ap=self.epsb[:, :], constant=EPS)
            wq = sb("wq", [128, 8, 2048], BF16)
            k1T = sb("k1T", [128, 128], BF16); k2T = sb("k2T", [128, 128], BF16)
            nfb = sb("nfb", [128, D]); nfin = sb("nfin", [128, D])
            G = sb("G", [128, 128, PBT], BF16)
            uTs = [sb(f"uTs{i}", [128, GE, 1024], BF16) for i in range(2)]
            vs = [sb(f"vs{i}", [128, GE, 1024], BF16) for i in range(2)]
            xp = [sb(f"xp{i}", [128, D]) for i in range(4)]
            if split:
                xalt = sb("xalt", [128, D]); selb = sb("selb", [128, 2])
                S.dma("sp", selb[:, :], din["sel"].v(din["sel"].tensor[0:1, :].partition_broadcast(128)))
            xn = sb("xn", [128, D], BF16); sq = sb("sq", [128, D], BF16); ss = sb("ss", [128, 2])
            xnT = [sb(f"xnT{i}", [128, 8, PBT], BF16) for i in range(2)]
            qT = sb("qT", [128, 16, PBT], BF16)
            sc = sb("sc", [128, 16, 128])
            yo = View(sc, sc.tensor[:, 0:8, :].rearrange("p a b -> p (a b)"))
            wk1 = sb("wk1", [128, 128]); wk2 = sb("wk2", [128, 128]); wk3 = sb("wk3", [128, 128])
            v1 = sb("v1", [128, 8, 16]); idx1 = sb("idx1", [128, 8, 16], U32); v2 = sb("v2", [128, 8, 24])
            cand = sb("cand", [128, 256]); cwk = sb("cwk", [128, 256]); cwk2 = sb("cwk2", [128, 256]); ctop = sb("ctop", [128, 8, 24])
            tau = sb("tau", [128, 8]); thr2 = sb("thr2", [128, 8]); ez = sb("ez", [128, 8, 16]); Z = sb("Z", [128, 8]); base = sb("base", [128, 8])
            thr = sb("thr", [128, 8, 16]); shf = sb("shf", [128, 8, 16]); idf_ = sb("idf_", [128, 8, 16])
            thrT = sb("thrT", [128, PBT]); shT = sb("shT", [128, PBT]); idT = sb("idT", [128, PBT], BF16)
            iob = sb("iob", [128, 128], BF16)
            S.op("dve", "tensor_copy", out=iob[:, :], in_=self.iota[:, :])
            q2rep = [sb(f"q2rep{i}", [128, 16, 128], BF16) for i in range(2)]
            dsh = [sb(f"dsh{i}", [128, 4, 128]) for i in range(2)]
            msk = [sb(f"msk{i}", [128, 4, 128], BF16) for i in range(2)]
            exv = [sb(f"exv{i}", [128, 4, 128], BF16) for i in range(2)]
            rhsG = [sb(f"rhsG{i}", [128, 4, 128], BF16) for i in range(2)]
            lhsG = [sb(f"lhsG{i}", [128, 4, 128], BF16) for i in range(2)]
            Hs = [sb(f"Hs{i}", [128, PBT], BF16) for i in range(2)]
            Wt = [sb(f"Wt{i}", [128, PBT], BF16) for i in range(2)]

            wqsrc = din["w_q"].tensor[l].rearrange("(c p) n -> p c n", p=128)
            for i in range(2):
                S.dma("pool", wq[:, :, i * 1024:(i + 1) * 1024], din["w_q"].v(wqsrc[:, :, i * 1024:(i + 1) * 1024]))
            S.dma("pool", k1T[:, :], din["k1T"].v(din["k1T"].tensor[l]))
            S.dma("pool", k2T[:, :], din["k2T"].v(din["k2T"].tensor[l]))
            S.dma("sp", nfb[:, :], din["norm_ffn"].v(din["norm_ffn"].tensor[l:l + 1, :].partition_broadcast(128)))
            S.dma("sp", nfin[:, :], din["norm_final"].v(din["norm_final"].tensor[0:1, :].partition_broadcast(128)))

            def bc3(buf, ap2, shape):
                return View(buf, ap2.unsqueeze(2).to_broadcast(shape))

            iota3 = View(iob, iob.tensor[:, :].unsqueeze(1).to_broadcast([128, 4, 128]))

            def front(pb):
                t0 = pb * PBT
                sl = pb % 2
                for tt in range(2):
                    xpt = xp[sl * 2 + tt]
                    S.dma("sp", xpt[:, :], self.xs[t0 + tt * 128:t0 + (tt + 1) * 128, :])
                    if split:
                        S.dma("sp", xalt[:, :], self.xs[SEQ // 2 + t0 + tt * 128:SEQ // 2 + t0 + (tt + 1) * 128, :])
                        S.op("dve", "tensor_scalar", out=xpt[:, :], in0=xpt[:, :], scalar1=selb[:, 0:1], scalar2=None, op0=ALU.mult)
                        S.op("dve", "scalar_tensor_tensor", out=xpt[:, :], in0=xalt[:, :], scalar=selb[:, 1:2], in1=xpt[:, :], op0=ALU.mult, op1=ALU.add)
                    self.rmsnorm_tile(xpt[:, :], nfb[:, :], xn[:, :], sq[:, :], ss)
                    pv = self.pbf(tt % 2)
                    for c in range(8):
                        S.op("pe", "transpose", out=pv[:, c * 128:(c + 1) * 128], in_=xn[:, c * 128:(c + 1) * 128], identity=self.idb[:, :])
                    S.op("dve", "tensor_copy", out=xnT[sl][:, :, tt * 128:(tt + 1) * 128], in_=pv.rearrange("p (c t) -> p c t", t=128))
                    yield
                for j in range(16):
                    h, side = j // 2, j % 2
                    bank = j % 2
                    for c in range(8):
                        S.op("pe", "matmul", out=PB[bank][:, 0:PBT], lhsT=wq[:, c, j * 128:(j + 1) * 128], rhs=xnT[sl][:, c, :], start=(c == 0), stop=(c == 7))
                    S.op("dve", "tensor_copy", out=qT[:, side * 8 + h, :], in_=PB[bank][:, 0:PBT])
                    yield
                for tt in range(2):
                    ts_ = slice(tt * 128, (tt + 1) * 128)
                    for half in range(2):
                        for jj in range(8):
                            j = half * 8 + jj
                            S.op("pe", "matmul", out=PB[jj // 4][:, (jj % 4) * 128:(jj % 4 + 1) * 128], lhsT=qT[:, j, ts_],
                                 rhs=(k1T if j < 8 else k2T)[:, :], start=True, stop=True)
                        for b_ in range(2):
                            dst = sc[:, half * 8 + 4 * b_:half * 8 + 4 * b_ + 4, :]
                            S.op("dve", "tensor_copy", out=dst, in_=PB[b_][:, :].rearrange("p (j n) -> p j n", n=128))
                        yield
                    for h in range(8):
                        s1 = sc[:, h, :]; s2 = sc[:, 8 + h, :]
                        S.op("dve", "max", out=v1[:, h, 0:8], in_=s1)
                        S.op("dve", "max", out=v2[:, h, 0:8], in_=s2)
                        S.op("dve", "max_index", out=idx1[:, h, 0:8], in_max=v1[:, h, 0:8], in_values=s1)
                        S.op("dve", "match_replace", out=wk1[:, :], in_to_replace=v1[:, h, 0:8], in_values=s1, imm_value=-1e30)
                        S.op("dve", "match_replace", out=wk2[:, :], in_to_replace=v2[:, h, 0:8], in_values=s2, imm_value=-1e30)
                        S.op("dve", "max", out=v1[:, h, 8:16], in_=wk1[:, :])
                        S.op("dve", "max", out=v2[:, h, 8:16], in_=wk2[:, :])
                        S.op("dve", "max_index", out=idx1[:, h, 8:16], in_max=v1[:, h, 8:16], in_values=wk1[:, :])
                        S.op("dve", "match_replace", out=wk3[:, :], in_to_replace=v2[:, h, 8:16], in_values=wk2[:, :], imm_value=-1e30)
                        S.op("dve", "tensor_tensor", out=cand[:, :].rearrange("p (i j) -> p i j", j=16),
                             in0=View(v1, v1.tensor[:, h, :].unsqueeze(2).to_broadcast([128, 16, 16])),
                             in1=View(v2, v2.tensor[:, h, 0:16].unsqueeze(1).to_broadcast([128, 16, 16])), op=ALU.add)
                        S.op("dve", "max", out=v2[:, h, 16:24], in_=wk3[:, :])
                        S.op("dve", "max", out=ctop[:, h, 0:8], in_=cand[:, :])
                        S.op("dve", "match_replace", out=cwk[:, :], in_to_replace=ctop[:, h, 0:8], in_values=cand[:, :], imm_value=-1e30)
                        S.op("dve", "max", out=ctop[:, h, 8:16], in_=cwk[:, :])
                        S.op("dve", "match_replace", out=cwk2[:, :], in_to_replace=ctop[:, h, 8:16], in_values=cwk[:, :], imm_value=-1e30)
                        S.op("dve", "max", out=ctop[:, h, 16:24], in_=cwk2[:, :])
                        yield
                    S.op("dve", "tensor_tensor", out=tau[:, :], in0=ctop[:, :, 15], in1=ctop[:, :, 16], op=ALU.add)
                    S.op("dve", "tensor_scalar", out=tau[:, :], in0=tau[:, :], scalar1=0.5, scalar2=None, op0=ALU.mult)
                    S.op("dve", "tensor_tensor", out=thr2[:, :], in0=v2[:, :, 15], in1=v2[:, :, 16], op=ALU.add)
                    S.op("dve", "tensor_scalar", out=thr2[:, :], in0=thr2[:, :], scalar1=0.5, scalar2=None, op0=ALU.mult)
                    S.op("dve", "tensor_tensor", out=ez[:, :, :], in0=ctop[:, :, 0:16], in1=bc3(ctop, ctop.tensor[:, :, 0], [128, 8, 16]), op=ALU.subtract)
                    S.op("act", "activation", out=ez[:, :, :], in_=ez[:, :, :], func=AF.Exp)
                    S.op("dve", "reduce_sum", out=Z[:, :], in_=ez[:, :, :], axis=AX.X)
                    S.op("act", "activation", out=Z[:, :], in_=Z[:, :], func=AF.Ln)
                    S.op("dve", "tensor_tensor", out=base[:, :], in0=ctop[:, :, 0], in1=Z[:, :], op=ALU.add)
                    S.op("dve", "tensor_tensor", out=thr[:, :, :], in0=bc3(tau, tau.tensor[:, :], [128, 8, 16]), in1=v1[:, :, :], op=ALU.subtract)
                    S.op("dve", "tensor_tensor", out=thr[:, :, :], in0=thr[:, :, :], in1=bc3(thr2, thr2.tensor[:, :], [128, 8, 16]), op=ALU.max)
                    S.op("dve", "tensor_tensor", out=shf[:, :, :], in0=bc3(base, base.tensor[:, :], [128, 8, 16]), in1=v1[:, :, :], op=ALU.subtract)
                    S.op("dve", "tensor_copy", out=idf_[:, :, :], in_=idx1[:, :, :])
                    for bnk, (srcb, dstb) in enumerate(((thr, thrT), (shf, shT), (idf_, idT))):
                        S.op("pe", "transpose", out=PB[bnk % 2][:, 0:128], in_=srcb[:, :, :].rearrange("p h i -> p (h i)"), identity=self.idf[:, :])
                        S.op("dve", "tensor_copy", out=dstb[:, ts_], in_=PB[bnk % 2][:, 0:128])
                    yield

            def gating(pb):
                NG = PBT // 4

                def stA(n):
                    if n % 4 == 0:
                        tg = n * 4
                        S.op("act", "activation", out=q2rep[(n // 4) % 2][:, :, :].rearrange("p t (h i) -> p t h i", i=16),
                             in_=View(qT, qT.tensor[:, 8:16, tg:tg + 16].rearrange("p h t -> p t h").unsqueeze(3).to_broadcast([128, 16, 8, 16])), func=AF.Copy)
                    qr = q2rep[(n // 4) % 2]
                    for i in range(4):
                        S.op("pe", "matmul", out=PB[n % 2][:, i * 128:(i + 1) * 128], lhsT=qr[:, (n % 4) * 4 + i, :], rhs=k2T[:, :], start=True, stop=True)

                def stB(n):
                    tq, k = n * 4, n % 2
                    p3 = PB[n % 2][:, :].rearrange("p (t e) -> p t e", e=128)
                    S.op("dve", "tensor_tensor", out=dsh[k][:, :, :], in0=p3, in1=bc3(shT, shT.tensor[:, tq:tq + 4], [128, 4, 128]), op=ALU.subtract)
                    S.op("dve", "tensor_tensor", out=msk[k][:, :, :], in0=p3, in1=bc3(thrT, thrT.tensor[:, tq:tq + 4], [128, 4, 128]), op=ALU.is_ge)
                    S.op("act", "activation", out=exv[k][:, :, :], in_=dsh[k][:, :, :], func=AF.Exp)
                    S.op("dve", "tensor_tensor", out=lhsG[k][:, :, :], in0=iota3, in1=bc3(idT, idT.tensor[:, tq:tq + 4], [128, 4, 128]), op=ALU.is_equal)
                    S.op("pool", "tensor_tensor", out=rhsG[k][:, :, :], in0=exv[k][:, :, :], in1=msk[k][:, :, :], op=ALU.mult)

                def stC(n):
                    k = n % 2
                    for i in range(4):
                        S.op("pe", "matmul", out=PB[2 + n % 2][:, i * 128:(i + 1) * 128], lhsT=lhsG[k][:, i, :], rhs=rhsG[k][:, i, :], start=True, stop=True)

                def stD(n):
                    tq = n * 4
                    S.op("act", "activation", out=G[:, :, tq:tq + 4], in_=PB[2 + n % 2][:, :].rearrange("p (t e) -> p e t", e=128), func=AF.Copy)

                for i_ in range(NG + 3):
                    if i_ < NG:
                        stA(i_)
                    if 0 <= i_ - 1 < NG:
                        stB(i_ - 1)
                    if 0 <= i_ - 2 < NG:
                        stC(i_ - 2)
                    if 0 <= i_ - 3 < NG:
                        stD(i_ - 3)

            def expert(pb, gen):
                sl = pb % 2
                NGRP = 128 // GE

                def load_group(gi):
                    e0 = gi * GE
                    S.dma("sp", uTs[gi % 2][:, :, :], self.uTb.v(self.uTb.tensor[e0:e0 + GE].rearrange("g p n -> p g n")))
                    S.dma("sp", vs[gi % 2][:, :, :], self.vrb.v(self.vrb.tensor[e0:e0 + GE].rearrange("g p n -> p g n")))

                def stH(e2):
                    gi, e = e2 // GE, e2 % GE
                    ub = uTs[gi % 2]
                    hb = 2 + e2 % 2
                    for c in range(8):
                        S.op("pe", "matmul", out=PB[hb][:, 0:PBT], lhsT=ub[:, e, c * 128:(c + 1) * 128], rhs=xnT[sl][:, c, :], start=(c == 0), stop=(c == 7))
                    S.op("act", "activation", out=Hs[e2 % 2][:, :], in_=PB[hb][:, 0:PBT], func=AF.Gelu)
                    S.op("pool", "tensor_tensor", out=Wt[e2 % 2][:, :], in0=Hs[e2 % 2][:, :], in1=G[:, e2, :], op=ALU.mult)

                def stO(e2):
                    gi, e = e2 // GE, e2 % GE
                    vb = vs[gi % 2]
                    for tt in range(2):
                        for half in range(2):
                            S.op("pe", "matmul", out=PB[4 + tt * 2 + half][:, :], lhsT=Wt[e2 % 2][:, tt * 128:(tt + 1) * 128],
                                 rhs=vb[:, e, half * 512:(half + 1) * 512], start=(e2 == 0), stop=(e2 == 127))

                load_group(0)
                load_group(1)
                for e2 in range(129):
                    if e2 < 128:
                        stH(e2)
                    if e2 >= 1:
                        stO(e2 - 1)
                        if (e2 - 1) % GE == GE - 1:
                            gnext = (e2 - 1) // GE + 2
                            if gnext < NGRP:
                                load_group(gnext)
                    if gen is not None and e2 % 3 == 0 and e2 >= 9:
                        next(gen, None)
                if gen is not None:
                    for _ in gen:
                        pass

            def epilogue(pb):
                t0 = pb * PBT
                sl = pb % 2
                for tt in range(2):
                    xpt = xp[sl * 2 + tt]
                    for half in range(2):
                        hs = slice(half * 512, (half + 1) * 512)
                        S.op("dve", "tensor_tensor", out=xpt[:, hs], in0=PB[4 + tt * 2 + half][:, :], in1=xpt[:, hs], op=ALU.add)
                    rows = slice(t0 + tt * 128, t0 + (tt + 1) * 128)
                    if last:
                        self.rmsnorm_tile(xpt[:, :], nfin[:, :], yo, sq[:, :], ss)
                        S.dma("sp", self.out[rows, :], yo)
                    else:
                        S.dma("sp", self.xs[rows, :], xpt[:, :])

            for _ in front(0):
                pass
            for pb in range(nblk):
                gating(pb)
                gen = front(pb + 1) if pb + 1 < nblk else None
                expert(pb, gen)
                epilogue(pb)


def _rope_consts():
    inv = np.zeros((128,), np.float32); sgn = np.zeros((128,), np.float32)
    base = (np.float32(500000.0) ** (-np.arange(0, 16, 2, dtype=np.float32) / np.float32(16))).astype(np.float32)
    for r in range(128):
        d = r % 64
        if d < 16:
            inv[r] = base[d % 8]
            sgn[r] = -1.0 if d < 8 else 1.0
    return np.stack([inv, sgn], axis=1).astype(np.float32)


def prep_shared(inp):
    f = lambda a: np.ascontiguousarray(np.asarray(a, dtype=np.float32))
    w_in = f(inp["w_in"])
    perm = np.arange(1024)
    d = perm % 64
    perm = np.where(d < 8, perm + 8, np.where(d < 16, perm - 8, perm))
    w_full = np.concatenate([w_in, w_in[:, :, perm]], axis=2)
    lam = np.concatenate([f(inp["lam_q1"]), f(inp["lam_k1"]), f(inp["lam_q2"]), f(inp["lam_k2"])], axis=1)
    cw = f(inp["conv_w"])
    cw = cw.transpose(0, 2, 1).reshape(L, 2, 128, 3).transpose(0, 2, 1, 3).reshape(L, 128, 6)
    gate = np.concatenate([f(inp["gla_w_gate2"]), f(inp["gla_b_gate"])[:, None, :]], axis=1)
    u = f(inp["peer_u"]).reshape(L, 128, 128, 8, 128)
    uT = np.ascontiguousarray(u.transpose(0, 2, 4, 3, 1)).reshape(L, 128, 128, 1024)
    v = f(inp["peer_v"]).reshape(L, 128, 128, 1024)
    vr = np.ascontiguousarray(v.transpose(0, 2, 1, 3))
    return {
        "norm_mix": f(inp["norm_mix"]), "norm_ffn": f(inp["norm_ffn"]), "norm_final": f(inp["norm_final"]).reshape(1, D),
        "w_in": np.ascontiguousarray(w_full), "lam": np.ascontiguousarray(lam), "diff_norm": f(inp["diff_norm"]),
        "conv_w": np.ascontiguousarray(cw), "gate_aug": np.ascontiguousarray(gate), "gla_norm": f(inp["gla_norm"]),
        "w_out": f(inp["w_out"]), "w_q": f(inp["peer_w_q"]),
        "k1T": np.ascontiguousarray(f(inp["peer_keys1"]).transpose(0, 2, 1)),
        "k2T": np.ascontiguousarray(f(inp["peer_keys2"]).transpose(0, 2, 1)),
        "uT": uT, "vr": vr, "rope_c": _rope_consts(),
    }


def core_inputs(shared, inp, b):
    m = dict(shared)
    m["x"] = np.ascontiguousarray(np.asarray(inp["x"], dtype=np.float32)[b])
    m["pos"] = np.ascontiguousarray(np.asarray(inp["positions"]).astype(np.int32)[b:b + 1])
    m["sel"] = np.array([[1.0, 0.0]], np.float32)
    return m


_PROG = None


def kernel(**inputs):
    global _PROG
    if _PROG is None:
        _PROG = Prog()
    shared = prep_shared(inputs)
    B = np.asarray(inputs["x"]).shape[0]
    in_maps = [core_inputs(shared, inputs, c % B) for c in range(8)]
    for c in range(B, 8):
        in_maps[c]["sel"] = np.array([[0.0, 1.0]], np.float32)
    res = run_bass_kernel_spmd(_PROG.nc, in_maps, core_ids=list(range(8)))
    out = np.stack([np.concatenate([np.asarray(res.results[b]["out"]), np.asarray(res.results[b + B]["out"])], axis=0)
                    for b in range(B)], axis=0)
    return out.astype(np.float32)
```
